# Optimizing a Trainium2 kernel written in Bass

```python
import jax, jax.numpy as jnp
from jax import lax
import numpy as np

D_MODEL = 4096
BATCH = 4
SEQ = 2048
DEPTH = 1
DEC_BATCH = 128
DEC_SEQ = 8
PAST_LEN = 16384
PAGE_SIZE = 128

HEAD_SIZE = 64
D_RWKV = D_MODEL // 2
D_CONV = D_MODEL - D_RWKV
N_HEADS = D_RWKV // HEAD_SIZE
DECAY_LORA = 96
AAA_LORA = 96
GATE_LORA = 256
RWKV_COLS = 3 * D_RWKV + DECAY_LORA + AAA_LORA + GATE_LORA
IN_COLS = RWKV_COLS + 2 * D_CONV
CONV_WIDTH = 31
CONV_BUF = CONV_WIDTH - 1
D_FF = ((8 * D_MODEL // 3 + 255) // 256) * 256
PLE_DIM = 256
RMS_EPS = 1e-6
LN_EPS = 1e-5
GN_EPS = 64e-5

kernel_name = 'rwkv7_conformer_conv_parallel_heads_decoder'


def rmsnorm(x, g):
    x32 = x.astype(jnp.float32)
    y = x32 * lax.rsqrt(jnp.mean(x32 * x32, axis=-1, keepdims=True) + RMS_EPS)
    return (y * g.astype(jnp.float32)).astype(x.dtype)


def layernorm_f32(x, w, b, eps):
    x32 = x.astype(jnp.float32)
    mu = jnp.mean(x32, axis=-1, keepdims=True)
    var = jnp.mean(jnp.square(x32 - mu), axis=-1, keepdims=True)
    return (x32 - mu) * lax.rsqrt(var + eps) * w.astype(jnp.float32) + b.astype(jnp.float32)


def wkv_recurrence(S0, r, w, k, v, kk, a):
    def step(S, inp):
        r_t, w_t, k_t, v_t, kk_t, a_t = inp
        s_kk = jnp.einsum('bhij,bhj->bhi', S, kk_t)
        S = (S * w_t[:, :, None, :]
             - s_kk[..., :, None] * (kk_t * a_t)[..., None, :]
             + v_t[..., :, None] * k_t[..., None, :])
        y_t = jnp.einsum('bhij,bhj->bhi', S, r_t)
        return S, y_t
    xs = tuple(jnp.swapaxes(t, 0, 1) for t in (r, w, k, v, kk, a))
    S, ys = lax.scan(step, S0, xs)
    return S, jnp.swapaxes(ys, 0, 1)


def decoder_layer(h, pe, wkv0, shift0, conv0,
                  g_mix, w_in, mu_shift, w0, w2, a0, a2, g2, k_k, k_a, r_k, lnx_w, lnx_b,
                  conv_w, conv_b, conv_ln_w, conv_ln_b, w_out,
                  g_ffn, w_gate_up, w_down, g_ple, w_ple_gate, w_ple_proj):
    f32 = jnp.float32
    Bn, T, _ = h.shape
    xn = rmsnorm(h, g_mix)
    z = jnp.einsum('btd,dc->btc', xn, w_in)
    zr = z[..., :RWKV_COLS]
    zc = z[..., RWKV_COLS:]
    z_first = jnp.einsum('bd,dc->bc', shift0.astype(xn.dtype), w_in[:, :RWKV_COLS])
    z_prev = jnp.concatenate([z_first[:, None, :], zr[:, :-1, :]], axis=1)
    zm = zr + (z_prev - zr) * mu_shift
    cuts = [D_RWKV, 2 * D_RWKV, 3 * D_RWKV, 3 * D_RWKV + DECAY_LORA, 3 * D_RWKV + DECAY_LORA + AAA_LORA]
    r, k, v, zw, za, zg = jnp.split(zm, cuts, axis=-1)
    w_log = -jax.nn.softplus(-(w0 + jnp.tanh(zw) @ w2).astype(f32)) - 0.5
    decay = jnp.exp(-jnp.exp(w_log))
    a = jax.nn.sigmoid((a0 + za @ a2).astype(f32))
    g = jax.nn.sigmoid(zg) @ g2
    heads = lambda t: t.astype(f32).reshape(Bn, T, N_HEADS, HEAD_SIZE)
    r, k, v, decay, a = heads(r), heads(k), heads(v), heads(decay), heads(a)
    k_k_h = k_k.astype(f32).reshape(N_HEADS, HEAD_SIZE)
    k_a_h = k_a.astype(f32).reshape(N_HEADS, HEAD_SIZE)
    kk = k * k_k_h
    kk = kk / jnp.maximum(jnp.sqrt(jnp.sum(kk * kk, axis=-1, keepdims=True)), 1e-12)
    k = k * (1.0 + (a - 1.0) * k_a_h)
    S, y = wkv_recurrence(wkv0.astype(f32), r, decay, k, v, kk, a)
    ym = jnp.mean(y, axis=-1, keepdims=True)
    yv = jnp.mean(jnp.square(y - ym), axis=-1, keepdims=True)
    y = ((y - ym) * lax.rsqrt(yv + GN_EPS)).reshape(Bn, T, D_RWKV) * lnx_w.astype(f32) + lnx_b.astype(f32)
    bonus = jnp.sum(r * k * r_k.astype(f32), axis=-1, keepdims=True) * v
    y_rwkv = ((y + bonus.reshape(Bn, T, D_RWKV)) * g.astype(f32)).astype(h.dtype)
    u = zc[..., :D_CONV] * jax.nn.sigmoid(zc[..., D_CONV:])
    full = jnp.concatenate([conv0.astype(u.dtype), u], axis=1)
    c = lax.conv_general_dilated(full, conv_w[:, None, :].astype(full.dtype), (1,), 'VALID',
                                 dimension_numbers=('NWC', 'WIO', 'NWC'),
                                 feature_group_count=D_CONV) + conv_b
    y_conv = jax.nn.silu(layernorm_f32(c, conv_ln_w, conv_ln_b, LN_EPS)).astype(h.dtype)
    h = h + jnp.einsum('btc,cd->btd', jnp.concatenate([y_rwkv, y_conv], axis=-1), w_out)
    gu = jnp.einsum('btd,df->btf', rmsnorm(h, g_ffn), w_gate_up)
    gate, up = jnp.split(gu, 2, axis=-1)
    h = h + jnp.einsum('btf,fd->btd', jax.nn.silu(gate) * up, w_down)
    pg = jax.nn.sigmoid(jnp.einsum('btd,de->bte', rmsnorm(h, g_ple), w_ple_gate))
    h = h + pg * jnp.einsum('btp,pd->btd', pe.astype(h.dtype), w_ple_proj)
    new_wkv = S.astype(h.dtype)
    new_shift = xn[:, -1, :]
    new_conv = full[:, -CONV_BUF:, :]
    return h, new_wkv, new_shift, new_conv


def setup_inputs(seed: int = 0) -> dict:
    key = jax.random.key(seed)
    ks = jax.random.split(key, 40)
    f32 = jnp.float32
    nrm = lambda kk, shape, s: s * jax.random.normal(kk, shape, f32)
    L = DEPTH
    return {
        'x_prompt': nrm(ks[0], (BATCH, SEQ, D_MODEL), 1.0),
        'x_sample': nrm(ks[1], (DEC_BATCH, DEC_SEQ, D_MODEL), 1.0),
        'state_wkv': nrm(ks[2], (L, DEC_BATCH, N_HEADS, HEAD_SIZE, HEAD_SIZE), 0.3),
        'state_shift': nrm(ks[3], (L, DEC_BATCH, D_MODEL), 1.0),
        'state_conv': nrm(ks[4], (L, DEC_BATCH, CONV_BUF, D_CONV), 0.5),
        'p_prompt': nrm(ks[5], (L, BATCH, SEQ, PLE_DIM), 1.0),
        'p_sample': nrm(ks[6], (L, DEC_BATCH, DEC_SEQ, PLE_DIM), 1.0),
        'g_mix': 1.0 + nrm(ks[7], (L, D_MODEL), 0.05),
        'w_in': nrm(ks[8], (L, D_MODEL, IN_COLS), D_MODEL ** -0.5),
        'mu_shift': jax.random.uniform(ks[9], (L, RWKV_COLS), f32),
        'w0': jax.random.uniform(ks[10], (L, D_RWKV), f32, -6.0, 1.0),
        'w2': nrm(ks[11], (L, DECAY_LORA, D_RWKV), 0.5 * DECAY_LORA ** -0.5),
        'a0': nrm(ks[12], (L, D_RWKV), 0.5),
        'a2': nrm(ks[13], (L, AAA_LORA, D_RWKV), 0.5 * AAA_LORA ** -0.5),
        'g2': nrm(ks[14], (L, GATE_LORA, D_RWKV), GATE_LORA ** -0.5),
        'k_k': 0.85 + nrm(ks[15], (L, D_RWKV), 0.05),
        'k_a': 1.0 + nrm(ks[16], (L, D_RWKV), 0.05),
        'r_k': nrm(ks[17], (L, N_HEADS, HEAD_SIZE), 0.1),
        'lnx_w': 1.0 + nrm(ks[18], (L, D_RWKV), 0.05),
        'lnx_b': nrm(ks[19], (L, D_RWKV), 0.01),
        'conv_w': nrm(ks[20], (L, CONV_WIDTH, D_CONV), CONV_WIDTH ** -0.5),
        'conv_b': nrm(ks[21], (L, D_CONV), 0.01),
        'conv_ln_w': 1.0 + nrm(ks[22], (L, D_CONV), 0.05),
        'conv_ln_b': nrm(ks[23], (L, D_CONV), 0.01),
        'w_out': nrm(ks[24], (L, D_MODEL, D_MODEL), D_MODEL ** -0.5),
        'g_ffn': 1.0 + nrm(ks[25], (L, D_MODEL), 0.05),
        'w_gate_up': nrm(ks[26], (L, D_MODEL, 2 * D_FF), D_MODEL ** -0.5),
        'w_down': nrm(ks[27], (L, D_FF, D_MODEL), D_FF ** -0.5),
        'g_ple': 1.0 + nrm(ks[28], (L, D_MODEL), 0.05),
        'w_ple_gate': nrm(ks[29], (L, D_MODEL, D_MODEL), D_MODEL ** -0.5),
        'w_ple_proj': nrm(ks[30], (L, PLE_DIM, D_MODEL), 0.5 * PLE_DIM ** -0.5),
        'g_final': 1.0 + nrm(ks[31], (D_MODEL,), 0.05),
    }


def reference(x_prompt, x_sample, state_wkv, state_shift, state_conv, p_prompt, p_sample,
              g_mix, w_in, mu_shift, w0, w2, a0, a2, g2, k_k, k_a, r_k, lnx_w, lnx_b,
              conv_w, conv_b, conv_ln_w, conv_ln_b, w_out, g_ffn, w_gate_up, w_down,
              g_ple, w_ple_gate, w_ple_proj, g_final):
    params = (g_mix, w_in, mu_shift, w0, w2, a0, a2, g2, k_k, k_a, r_k, lnx_w, lnx_b,
              conv_w, conv_b, conv_ln_w, conv_ln_b, w_out, g_ffn, w_gate_up, w_down,
              g_ple, w_ple_gate, w_ple_proj)
    dt = x_prompt.dtype
    hp, hs = x_prompt, x_sample
    wkv_p, shift_p, conv_p, wkv_s, shift_s, conv_s = [], [], [], [], [], []
    for i in range(DEPTH):
        lp = [t[i] for t in params]
        hp, s_w, s_sh, s_c = decoder_layer(
            hp, p_prompt[i],
            jnp.zeros((BATCH, N_HEADS, HEAD_SIZE, HEAD_SIZE), dt),
            jnp.zeros((BATCH, D_MODEL), dt),
            jnp.zeros((BATCH, CONV_BUF, D_CONV), dt), *lp)
        wkv_p.append(s_w); shift_p.append(s_sh); conv_p.append(s_c)
        hs, s_w, s_sh, s_c = decoder_layer(
            hs, p_sample[i], state_wkv[i], state_shift[i], state_conv[i], *lp)
        wkv_s.append(s_w); shift_s.append(s_sh); conv_s.append(s_c)
    y_prompt = rmsnorm(hp, g_final)
    y_sample = rmsnorm(hs, g_final)
    return (y_prompt, y_sample,
            jnp.stack(wkv_p), jnp.stack(shift_p), jnp.stack(conv_p),
            jnp.stack(wkv_s), jnp.stack(shift_s), jnp.stack(conv_s))
```

```python
import math
import numpy as np
from contextlib import ExitStack
import concourse.bass as bass
import concourse.mybir as mybir
from concourse.bass_utils import run_bass_kernel_spmd

F32 = mybir.dt.float32
BF16 = mybir.dt.bfloat16
AF = mybir.ActivationFunctionType
ALU = mybir.AluOpType
AX = mybir.AxisListType

D = 4096
DR = 2048
NHP = 16
DFF = 11008
NFC = 86
PLE = 256
RW = 6592
INC = 10688
TP = 1024
TS = 128
TM = TP + TS
NSEG = 16
EXM = 32 + TP + NSEG * 9
EXP = 32 + TP
STRICT = True
NPH = 7
SUB = 0
NCORES = 8

PP = {}
_o = 0
for _n, _w in [("mu_r", 16), ("mu_k", 16), ("mu_v", 16), ("mu_w", 1), ("mu_a", 1), ("mu_g", 2),
               ("w0", 16), ("a0", 16), ("k_k", 16), ("k_a", 16), ("r_k", 16), ("lnx_w", 16),
               ("lnx_b", 16), ("conv_b", 16), ("cln_w", 16), ("cln_b", 16), ("conv_w", 16 * 31),
               ("eps", 4)]:
    PP[_n] = (_o, _w)
    _o += _w
NPP = _o


class Prog:
    ENG = ("pe", "act", "dve", "pool", "sp")

    def __init__(self, nc, es):
        self.nc = nc
        self.es = es
        self.e = dict(pe=nc.tensor, act=nc.scalar, dve=nc.vector, pool=nc.gpsimd, sp=nc.sync)
        self.sem = {k: es.enter_context(nc.semaphore("s_" + k)) for k in self.ENG}
        self.cnt = {k: 0 for k in self.ENG}
        self.seen = {k: {} for k in self.ENG}
        self.dsem = {}
        self.dcnt = {}
        self.lastw = {}
        self.readers = {}
        self.nops = 0
        self.n2p = {}
        self.nq = {}
        self.swq = []

    def _wait(self, eng, tok):
        if tok is None:
            return
        kind, key, val = tok
        if kind == "e" and key == eng and (eng == "pe" or not STRICT):
            return
        if self.seen[eng].get((kind, key), 0) >= val:
            return
        s = self.sem[key] if kind == "e" else self.dsem[key]
        self.e[eng].wait_ge(s, val)
        self.seen[eng][(kind, key)] = val

    def _deps(self, eng, r, w):
        for k in list(r) + list(w):
            self._wait(eng, self.lastw.get(k))
        for k in w:
            for tok in self.readers.get(k, {}).values():
                self._wait(eng, tok)

    def _commit(self, tok, r, w):
        for k in w:
            self.lastw[k] = tok
            self.readers[k] = {}
        for k in r:
            self.readers.setdefault(k, {})[(tok[0], tok[1])] = tok

    def op(self, eng, fn, r=(), w=()):
        psr = [k for k in r if k.startswith("ps")]
        if psr:
            r = [k for k in r if not k.startswith("ps")]
            w = list(w) + psr
        self._deps(eng, r, w)
        inst = fn(self.e[eng])
        self.cnt[eng] += 1
        inst.then_inc(self.sem[eng], 1)
        self._commit(("e", eng, self.cnt[eng]), r, w)
        self.nops += 1

    def dma(self, q, out, in_, r=(), w=(), sem=None):
        semname = self.semname(sem or ("d_" + (w[0] if w else r[0])), q)
        self._deps(q, r, w)
        if self.dcnt[semname] > 0:
            self._wait(q, ("d", semname, self.dcnt[semname]))
        if q == "pool":
            nd = 1
            for d_ in list(out.shape)[:-1]:
                nd *= int(d_)
            nd = nd // 16 + 3
            while self.swq and sum(x[2] for x in self.swq) + nd > 760:
                sn_, cv_, _ = self.swq.pop(0)
                self._wait(q, ("d", sn_, cv_))
            self.swq.append((semname, self.dcnt[semname] + 16, nd))
        self.e[q].dma_start(out=out, in_=in_).then_inc(self.dsem[semname], 16)
        self.dcnt[semname] += 16
        self._commit(("d", semname, self.dcnt[semname]), r, w)
        self.nops += 1

    def semname(self, name, q):
        if name not in self.n2p:
            self.nq[q] = self.nq.get(q, 0) + 1
            if q == "pool":
                pname = "dq%d" % (self.nq[q] % 30)
            else:
                pname = "dp%d" % (self.nq[q] % 60)
            if pname not in self.dsem:
                self.dsem[pname] = self.es.enter_context(self.nc.semaphore(pname))
                self.dcnt[pname] = 0
            self.n2p[name] = pname
        return self.n2p[name]

    def barrier(self):
        for eng in self.ENG:
            for e2 in self.ENG:
                if e2 != eng and self.cnt[e2] > 0:
                    self._wait(eng, ("e", e2, self.cnt[e2]))
            for sn, v in self.dcnt.items():
                if v > 0:
                    self._wait(eng, ("d", sn, v))
        self.lastw = {}
        self.readers = {}


def build():
    nc = bass.Bass("TRN2", target_bir_lowering=False)

    def din(name, shape, dt=F32):
        return nc.dram_tensor(name, list(shape), dt, kind="ExternalInput").ap()

    def dout(name, shape, dt=F32):
        return nc.dram_tensor(name, list(shape), dt, kind="ExternalOutput").ap()

    def dscr(name, shape, dt=F32):
        return nc.dram_tensor(name, list(shape), dt).ap()

    x_main = din("x_main", [TM, D])
    x_pre = din("x_pre", [TP, D])
    p_main = din("p_main", [TM, PLE])
    sshift = din("sshift", [NSEG, D])
    sconv = din("sconv", [NSEG * 30, DR])
    swkvT = din("swkvT", [NHP, 128, NSEG, 64])
    w_in = din("w_in", [D, INC])
    w2 = din("w2", [96, DR])
    a2 = din("a2", [96, DR])
    g2 = din("g2", [256, DR])
    w_out = din("w_out", [D, D])
    w_gu = din("w_gu", [D, 2 * DFF])
    w_dn = din("w_dn", [DFF, D])
    w_pg = din("w_pg", [D, D])
    w_pp = din("w_pp", [PLE, D])
    gvec = din("gvec", [4, D])
    pp_d = din("pp", [128, NPP])
    ident_d = din("ident", [128, 128])
    blk_d = din("blk", [128, 128])
    blks_d = din("blks", [128, 128])
    onesc_d = din("onesc", [128, 128])
    maskg_d = din("maskg", [2, 128, 4 * 128])
    maskl_d = din("maskl", [2, 128, 2 * 128])
    segf_d = din("segf", [128, NSEG * 128])
    segt_d = din("segt", [128, NSEG])
    rmask_d = din("rmask", [128, TM])

    y_out = dout("y_out", [TM, D])
    shift_out = dout("shift_out", [NSEG + 1, D])
    convp_out = dout("convp_out", [30, DR])
    convs_out = dout("convs_out", [NSEG, 30, DR])
    wkvp_out = dout("wkvp_out", [NHP, 128, 64])
    wkvs_out = dout("wkvs_out", [NHP, 128, NSEG, 64])

    catT = dscr("catT", [D, TM], BF16)
    cTd = dscr("cTd", [DR, TM], F32)
    h1d = dscr("h1d", [TM, D], F32)
    actT = dscr("actT", [DFF, TM], BF16)
    h2d = dscr("h2d", [TM, D], F32)
    h3d = dscr("h3d", [TM, D], F32)

    es = ExitStack()
    P = Prog(nc, es)

    def sb(st, name, shape, dt=F32):
        return st.enter_context(nc.sbuf_tensor("t_" + name, list(shape), dt))

    zA = es.enter_context(nc.psum_tensor("zA", [128, 1536], F32))
    zB = es.enter_context(nc.psum_tensor("zB", [128, 1536], F32))
    m0 = es.enter_context(nc.psum_tensor("m0", [128, 512], F32))
    m1 = es.enter_context(nc.psum_tensor("m1", [128, 512], F32))

    def bank(i):
        if i < 3:
            return zA[:, i * 512:(i + 1) * 512], "ps%d" % i
        if i < 6:
            return zB[:, (i - 3) * 512:(i - 2) * 512], "ps%d" % i
        return (m0 if i == 6 else m1)[:, :], "ps%d" % i

    ident_f = sb(es, "ident_f", [128, 128])
    ident_b = sb(es, "ident_b", [128, 128], BF16)
    pp = sb(es, "pp", [128, NPP])
    omka = sb(es, "omka", [128, 16])
    P.dma("sp", ident_f[:], ident_d, w=["ident_f"])
    P.dma("pool", ident_b[:], ident_d, w=["ident_b"])
    P.dma("sp", pp[:], pp_d, w=["pp"])

    def ppc(name, j=0, n=1, rows=128):
        o, w = PP[name]
        return pp[0:rows, o + j:o + j + n]

    P.op("dve", lambda e: e.tensor_scalar(out=omka[:], in0=pp[:, PP["k_a"][0]:PP["k_a"][0] + 16],
                                          scalar1=-1.0, scalar2=1.0, op0=ALU.mult, op1=ALU.add),
         r=["pp"], w=["omka"])
    EPS_RMS = lambda rows=128: ppc("eps", 0, 1, rows)
    EPS_GN = lambda rows=128: ppc("eps", 1, 1, rows)
    EPS_LN = lambda rows=128: ppc("eps", 2, 1, rows)


    def gemm_fm(wt, wkey, ncols, xT, xkey, E, bank0, nk=32, rows=128):
        nb = (E + 511) // 512
        for k in range(nk):
            for j in range(nb):
                c0 = j * 512
                c1 = min(E, c0 + 512)
                bap, bkey = bank(bank0 + j)
                P.op("pe", lambda e, k=k, c0=c0, c1=c1, bap=bap: e.matmul(out=bap[0:ncols, 0:c1 - c0], lhsT=wt[0:rows, k, 0:ncols], rhs=xT[0:rows, k, c0:c1], start=(k == 0), stop=(k == nk - 1)),
                     r=[wkey, xkey], w=[bkey])

    def zps(bank0, ncols, E):
        t = zA if bank0 == 0 else zB
        return t[0:ncols, 0:E], ["ps%d" % (bank0 + j) for j in range((E + 511) // 512)]

    wslots = []
    wstate = {"i": 0}

    def load_wcol(c0, ncols):
        i = wstate["i"] % len(wslots)
        wstate["i"] += 1
        t = wslots[i]
        key = "wslot%d" % i
        P.dma("pool", t[:, :, 0:ncols], w_in[:, c0:c0 + ncols].rearrange("(k p) n -> p k n", p=128), w=[key])
        return t, key

    S_f = sb(es, "S_f", [128, NHP, 64])
    histT = sb(es, "histT", [128, 32, 32], BF16)
    P.op("dve", lambda e: e.memset(S_f[:], 0.0), w=["S_f"])

    def mixer_pass(is_main):
        E = EXM if is_main else EXP
        TY = TM if is_main else TP
        nchunks = 9 if is_main else 8
        tag = "m" if is_main else "p"
        with ExitStack() as st:
            xT = sb(st, tag + "xT", [128, 32, E], BF16)
            xkey = tag + "xT"
            with ExitStack() as sa:
                if is_main:
                    P.op("dve", lambda e: e.tensor_copy(out=xT[:, :, 0:32], in_=histT[:]), r=["histT"], w=[xkey])

                    def dst_fn(tt, kk):
                        if tt < 8:
                            return xT[:, kk * 8:(kk + 1) * 8, 32 + tt * 128:32 + (tt + 1) * 128]
                        v = xT[:, kk * 8:(kk + 1) * 8, 32 + TP:E].rearrange("p q (s t) -> p q s t", t=9)[:, :, :, 1:9]
                        return v

                    def fp32_rows(tt, xn32, query):
                        if query:
                            return tt >= 7
                        if tt == 7:
                            P.dma("sp", shift_out[NSEG:NSEG + 1, :], xn32[127:128, :], r=["mA_xn32"], sem="d_shout")
                        else:
                            for sq_ in range(NSEG):
                                P.dma("sp", shift_out[sq_:sq_ + 1, :], xn32[sq_ * 8 + 7:sq_ * 8 + 8, :], r=["mA_xn32"], sem="d_shout%d" % (sq_ % 4))
                        return True

                    def src_rows(tt):
                        return x_main[tt * 128:(tt + 1) * 128, :]
                    norm_transpose_main(sa, src_rows, 9, 0, xT, xkey, dst_fn, "mA", fp32_rows)
                    norm_transpose_shift(sa, xT, xkey)
                else:
                    P.op("dve", lambda e: e.memset(xT[:, :, 0:32], 0.0), w=[xkey])

                    def dst_fn(tt, kk):
                        return xT[:, kk * 8:(kk + 1) * 8, 32 + tt * 128:32 + (tt + 1) * 128]
                    norm_transpose_main(sa, lambda tt: x_pre[tt * 128:(tt + 1) * 128, :], 8, 0, xT, xkey, dst_fn, "pA", None)
                    P.op("dve", lambda e: e.tensor_copy(out=histT[:], in_=xT[:, :, E - 32:E]), r=[xkey], w=["histT"])
                P.barrier()
            if SUB == 1 or SUB >= 11:
                return
            with ExitStack() as sbk:
                rwkv_phase(sbk, is_main, xT, xkey, E, TY, nchunks, tag)
                P.barrier()
            if is_main:
                with ExitStack() as sc:
                    conv_phase(sc, xT, xkey, E)
                    P.barrier()

    def norm_transpose_main(st, src_rows, ntiles, gcol, xT, xkey, dst_fn, tag, fp32_rows):
        def dst2(tt, kk):
            return dst_fn(tt, kk)
        norm_transpose_impl(st, src_rows, ntiles, gcol, xT, xkey, dst2, tag, fp32_rows, sample_tile=(8 if ntiles == 9 else None))

    def norm_transpose_impl(st, src_rows_fn, ntiles, gcol, xT, xT_key, dst_fn, tag, fp32_rows, sample_tile):
        gb = sb(st, tag + "_gb", [128, D])
        P.dma("sp", gb[:], gvec[gcol].partition_broadcast(128), w=[tag + "_gb"])
        xt = [sb(st, tag + "_xt%d" % i, [128, D]) for i in range(2)]
        sqf = sb(st, tag + "_sqf", [128, D])
        xnb = [sb(st, tag + "_xnb%d" % i, [128, D], BF16) for i in range(2)]
        xn32 = sb(st, tag + "_xn32", [128, D]) if fp32_rows is not None else None
        stt = sb(st, tag + "_stt", [128, 4 * ntiles])
        for tt in range(ntiles if SUB < 11 else SUB - 10):
            s = tt % 2
            xk = tag + "_xt%d" % s
            P.dma("sp", xt[s][:, :], src_rows_fn(tt), w=[xk])
            ssc = stt[:, 4 * tt:4 * tt + 1]
            stdc = stt[:, 4 * tt + 1:4 * tt + 2]
            rsc = stt[:, 4 * tt + 2:4 * tt + 3]
            sk = tag + "_stt%d" % tt
            P.op("act", lambda e, s=s: e.activation(out=sqf[:, :], in_=xt[s][:, :], func=AF.Square), r=[xk], w=[tag + "_sqf"])
            P.op("dve", lambda e, ssc=ssc: e.reduce_sum(out=ssc, in_=sqf[:, :], axis=AX.X), r=[tag + "_sqf"], w=[sk])
            P.op("act", lambda e, ssc=ssc, stdc=stdc: e.activation(out=stdc, in_=ssc, func=AF.Sqrt, bias=EPS_RMS(), scale=1.0 / D), r=[sk, "pp"], w=[sk])
            P.op("dve", lambda e, rsc=rsc, stdc=stdc: e.reciprocal(out=rsc, in_=stdc), r=[sk], w=[sk])
            P.op("dve", lambda e, s=s, rsc=rsc: e.scalar_tensor_tensor(out=xnb[s][:, :], in0=xt[s][:, :], scalar=rsc, in1=gb[:, :], op0=ALU.mult, op1=ALU.mult),
                 r=[xk, sk, tag + "_gb"], w=[tag + "_xnb%d" % s])
            if fp32_rows is not None and fp32_rows(tt, None, True):
                P.op("dve", lambda e, s=s, rsc=rsc: e.scalar_tensor_tensor(out=xn32[:, :], in0=xt[s][:, :], scalar=rsc, in1=gb[:, :], op0=ALU.mult, op1=ALU.mult),
                     r=[xk, sk, tag + "_gb"], w=[tag + "_xn32"])
                fp32_rows(tt, xn32, False)
            transpose_tile(xnb[s], tag + "_xnb%d" % s, 128, lambda kk, tt=tt: dst_fn(tt, kk), xT_key, is_sample=(tt == sample_tile))

    def transpose_tile(src, skey, rows, dst_of_kk, xT_key, is_sample=False, nk=32):
        for kk in range(nk // 8):
            bi = 6 + (kk % 2)
            bap, bkey = bank(bi)
            pT = bap.bitcast(BF16)
            for q in range(8):
                kc = kk * 8 + q
                P.op("pe", lambda e, kc=kc, q=q, pT=pT: e.transpose(out=pT[:, q * 128:q * 128 + rows], in_=src[0:rows, kc * 128:(kc + 1) * 128], identity=ident_b[0:rows, 0:rows]),
                     r=[skey, "ident_b"], w=[bkey])
            srcv = pT.rearrange("p (q t) -> p q t", q=8)[:, :, 0:rows]
            if is_sample:
                srcv = srcv.rearrange("p q (s t) -> p q s t", t=8)
            dst = dst_of_kk(kk)
            if kk % 2 == 0:
                P.op("act", lambda e, srcv=srcv, dst=dst: e.activation(out=dst, in_=srcv, func=AF.Copy), r=[bkey], w=[xT_key])
            else:
                P.op("dve", lambda e, srcv=srcv, dst=dst: e.tensor_copy(out=dst, in_=srcv), r=[bkey], w=[xT_key])

    def norm_transpose_shift(st, xT, xkey):
        t32 = sb(st, "sh32", [NSEG, D])
        tb = sb(st, "shb", [NSEG, D], BF16)
        P.dma("sp", t32[:], sshift, w=["sh32"])
        P.op("dve", lambda e: e.tensor_copy(out=tb[:], in_=t32[:]), r=["sh32"], w=["shb"])

        def dst(kk):
            return xT[:, kk * 8:(kk + 1) * 8, 32 + TP:EXM].rearrange("p q (s t) -> p q s t", t=9)[:, :, :, 0]
        transpose_tile(tb, "shb", NSEG, dst, xkey)

    def rwkv_phase(st, is_main, xT, xkey, E, TY, nchunks, tag):
        nonlocal wslots
        wslots = [sb(st, tag + "ws%d" % i, [128, 32, 128], BF16) for i in range(2)]
        wstate["i"] = 0
        NPC = 8
        w2bs = [sb(st, tag + "w2b%d" % i, [96, 128], BF16) for i in range(2)]
        a2bs = [sb(st, tag + "a2b%d" % i, [96, 128], BF16) for i in range(2)]
        g2bs = [sb(st, tag + "g2b%d" % i, [128, 2, 128], BF16) for i in range(2)]
        blk = sb(st, tag + "blk", [128, 128])
        blks = sb(st, tag + "blks", [128, 128])
        P.dma("sp", blk[:], blk_d, w=["blk"])
        P.dma("sp", blks[:], blks_d, w=["blks"])
        rmask = sb(st, tag + "rmask", [128, TY], BF16)
        P.dma("pool", rmask[:], rmask_d[:, 0:TY], w=["rmask"])
        maskg_p = sb(st, tag + "mgp", [128, 512], BF16)
        maskl_p = sb(st, tag + "mlp", [128, 256], BF16)
        P.dma("pool", maskg_p[:], maskg_d[0], w=["mgp"])
        P.dma("pool", maskl_p[:], maskl_d[0], w=["mlp"])
        if is_main:
            maskg_s = sb(st, tag + "mgs", [128, 512], BF16)
            maskl_s = sb(st, tag + "mls", [128, 256], BF16)
            segf = sb(st, tag + "segf", [128, NSEG, 128], BF16)
            segt = sb(st, tag + "segt", [128, NSEG], BF16)
            P.dma("pool", maskg_s[:], maskg_d[1], w=["mgs"])
            P.dma("pool", maskl_s[:], maskl_d[1], w=["mls"])
            P.dma("pool", segf[:], segf_d.rearrange("p (s t) -> p s t", s=NSEG), w=["segf"])
            P.dma("pool", segt[:], segt_d, w=["segt"])

        t2f = sb(st, tag + "t2f", [128, E])
        t3f = sb(st, tag + "t3f", [128, E])
        zs, dX = t2f, t3f

        def mix(zbank0, ncols, mu_ap, out_t, okey, func=None):
            zp, zkeys = zps(zbank0, ncols, E)
            P.op("act", lambda e: e.activation(out=zs[0:ncols, :], in_=zp, func=AF.Copy), r=zkeys, w=["t2"])
            P.op("dve", lambda e: e.tensor_tensor(out=dX[0:ncols, 0:E - 32], in0=zs[0:ncols, 31:E - 1], in1=zs[0:ncols, 32:E], op=ALU.subtract), r=["t2"], w=["t3"])
            dst = out_t if func is None else dX
            tgt = out_t if func is None else zs
            if func is None:
                P.op("dve", lambda e: e.scalar_tensor_tensor(out=out_t[0:ncols, 0:TP], in0=dX[0:ncols, 0:TP], scalar=mu_ap, in1=zs[0:ncols, 32:32 + TP], op0=ALU.mult, op1=ALU.add),
                     r=["t3", "t2", "pp"], w=[okey])
                if is_main:
                    P.op("dve", lambda e: e.scalar_tensor_tensor(
                        out=out_t[0:ncols, TP:TM].rearrange("p (s t) -> p s t", t=8),
                        in0=dX[0:ncols, TP:TP + 144].rearrange("p (s t) -> p s t", t=9)[:, :, 1:9], scalar=mu_ap,
                        in1=zs[0:ncols, 32 + TP:E].rearrange("p (s t) -> p s t", t=9)[:, :, 1:9], op0=ALU.mult, op1=ALU.add),
                        r=["t3", "t2", "pp"], w=[okey])
            else:
                P.op("dve", lambda e: e.scalar_tensor_tensor(out=dX[0:ncols, 0:E - 32], in0=dX[0:ncols, 0:E - 32], scalar=mu_ap, in1=zs[0:ncols, 32:E], op0=ALU.mult, op1=ALU.add),
                     r=["t3", "t2", "pp"], w=["t3"])
                P.op("act", lambda e: e.activation(out=out_t[0:ncols, 0:TP], in_=dX[0:ncols, 0:TP], func=func), r=["t3"], w=[okey])
                if is_main:
                    P.op("act", lambda e: e.activation(out=out_t[0:ncols, TP:TM].rearrange("p (s t) -> p s t", t=8),
                                                       in_=dX[0:ncols, TP:TP + 144].rearrange("p (s t) -> p s t", t=9)[:, :, 1:9], func=func), r=["t3"], w=[okey])

        twT = sb(st, tag + "twT", [96, TY], BF16)
        zaT = sb(st, tag + "zaT", [96, TY], BF16)
        sgT = sb(st, tag + "sgT", [128, 2, TY], BF16)
        lo = 3 * DR
        for (c0, ncols, mu_ap, out_v, okey, func, b0) in [
            (lo, 96, ppc("mu_w", 0, 1, 96), twT, "twT", AF.Tanh, 0),
            (lo + 96, 96, ppc("mu_a", 0, 1, 96), zaT, "zaT", AF.Copy, 3),
            (lo + 192, 128, ppc("mu_g", 0, 1), None, "sgT", AF.Sigmoid, 0),
            (lo + 320, 128, ppc("mu_g", 1, 1), None, "sgT", AF.Sigmoid, 3),
        ]:
            wt, wkey = load_wcol(c0, ncols)
            gemm_fm(wt, wkey, ncols, xT, xkey, E, b0)
            if out_v is None:
                kidx = 0 if c0 == lo + 192 else 1
                mix(b0, ncols, mu_ap, sgT[:, kidx, :], okey, func)
            else:
                mix(b0, ncols, mu_ap, out_v, okey, func)

        if SUB == 2:
            return
        def fm(name, dt=F32):
            return sb(st, tag + name, [128, TY], dt)
        r_t, k_t, v_t = fm("r_t"), fm("k_t"), fm("v_t")
        a_t, ld_t, cl_t = fm("a_t"), fm("ld_t"), fm("cl_t")
        t1 = fm("t1")
        t2, t3 = t2f[:, 0:TY], t3f[:, 0:TY]
        RT, KT, BT, KK = fm("RT", BF16), fm("KT", BF16), fm("BT", BF16), fm("KK", BF16)
        yT = a_t
        ob = KK
        tmpc = sb(st, tag + "tmpc", [128, 3, 128], BF16)
        NS = 5 if is_main else 8
        Gms = [[sb(st, tag + "Gm%d_%d" % (s_, h), [128, 4, 128], BF16) for h in range(2)] for s_ in range(NS)]
        lvs = [sb(st, tag + "lv%d" % s_, [128, 2, 3, 128], BF16) for s_ in range(NS)]
        tok3s = [sb(st, tag + "tok3_%d" % s_, [128, 3, 128], BF16) for s_ in range(NS)]
        RHSb = sb(st, tag + "RHSb", [128, 128], BF16)
        Ub = sb(st, tag + "Ub", [128, 128], BF16)
        S_b = sb(st, tag + "S_b", [128, 64], BF16)
        if is_main:
            KTseg = r_t[:, 0:1024].bitcast(BF16).rearrange("p (s t) -> p s t", s=NSEG)
            RTseg = KTseg
            BGseg = cl_t[:, 0:1024].bitcast(BF16).rearrange("p (s t) -> p s t", s=NSEG)
            KGseg = sb(st, tag + "KGseg", [128, NSEG, 128], BF16)
            Ss_f = sb(st, tag + "Ss_f", [128, NSEG, 64])
            Ss_n = Ss_f
            Ss_b = sb(st, tag + "Ss_b", [128, NSEG, 64], BF16)

        def ev(eng, fn, r, w):
            P.op(eng, fn, r=r, w=w)

        for hp in range(NHP):
            f0 = hp * 128
            lsl = hp % 2
            w2b, a2b, g2b = w2bs[lsl], a2bs[lsl], g2bs[lsl]
            w2k, a2k, g2k = "w2b%d" % lsl, "a2b%d" % lsl, "g2b%d" % lsl
            P.dma("pool", w2b[:, :], w2[:, f0:f0 + 128], w=[w2k])
            P.dma("pool", a2b[:, :], a2[:, f0:f0 + 128], w=[a2k])
            if is_main:
                P.dma("pool", g2b[:, :, :], g2[:, f0:f0 + 128].rearrange("(k p) n -> p k n", p=128), w=[g2k])
            for (nm, cbase, out_t, mu_nm, b0) in [("r", 0, r_t, "mu_r", 0), ("k", DR, k_t, "mu_k", 3), ("v", 2 * DR, v_t, "mu_v", 0)]:
                wt, wkey = load_wcol(cbase + f0, 128)
                gemm_fm(wt, wkey, 128, xT, xkey, E, b0)
                mix(b0, 128, ppc(mu_nm, hp, 1), out_t, nm + "_t")
            nbt = (TY + 511) // 512
            for j in range(nbt):
                c0, c1 = j * 512, min(TY, j * 512 + 512)
                bap, bkey = bank(3 + j)
                P.op("pe", lambda e, c0=c0, c1=c1, bap=bap: e.matmul(out=bap[:, 0:c1 - c0], lhsT=w2b[0:96, :], rhs=twT[0:96, c0:c1], start=True, stop=True), r=[w2k, "twT"], w=[bkey])
            zp, zkeys = zB[:, 0:TY], ["ps%d" % (3 + j) for j in range(nbt)]
            P.op("act", lambda e: e.activation(out=ld_t[:, :], in_=zp, func=AF.Sigmoid, bias=ppc("w0", hp, 1), scale=1.0), r=zkeys + ["pp"], w=["ld_t"])
            P.op("dve", lambda e: e.tensor_scalar(out=ld_t[:, :], in0=ld_t[:, :], scalar1=-math.exp(-0.5), scalar2=None, op0=ALU.mult), r=["ld_t"], w=["ld_t"])
            for j in range(nbt):
                c0, c1 = j * 512, min(TY, j * 512 + 512)
                bap, bkey = bank(0 + j)
                P.op("pe", lambda e, c0=c0, c1=c1, bap=bap: e.matmul(out=bap[:, 0:c1 - c0], lhsT=a2b[0:96, :], rhs=zaT[0:96, c0:c1], start=True, stop=True), r=[a2k, "zaT"], w=[bkey])
            zpa, zkeysa = zA[:, 0:TY], ["ps%d" % j for j in range(nbt)]
            P.op("act", lambda e: e.activation(out=a_t[:, :], in_=zpa, func=AF.Sigmoid, bias=ppc("a0", hp, 1), scale=1.0), r=zkeysa + ["pp"], w=["a_t"])
            P.op("dve", lambda e: e.tensor_scalar(out=t1[:, :], in0=k_t[:, :], scalar1=ppc("k_k", hp, 1), scalar2=None, op0=ALU.mult), r=["k_t", "pp"], w=["t1"])
            P.op("act", lambda e: e.activation(out=t2[:, :], in_=t1[:, :], func=AF.Square), r=["t1"], w=["t2"])
            for j in range(nbt):
                c0, c1 = j * 512, min(TY, j * 512 + 512)
                bap, bkey = bank(0 + j)
                P.op("pe", lambda e, c0=c0, c1=c1, bap=bap: e.matmul(out=bap[:, 0:c1 - c0], lhsT=blk[:, :], rhs=t2[:, c0:c1], start=True, stop=True), r=["blk", "t2"], w=[bkey])
            P.op("dve", lambda e: e.tensor_scalar(out=t2[:, :], in0=zpa, scalar1=1e-24, scalar2=None, op0=ALU.max), r=zkeysa, w=["t2"])
            P.op("act", lambda e: e.activation(out=t2[:, :], in_=t2[:, :], func=AF.Sqrt), r=["t2"], w=["t2"])
            P.op("dve", lambda e: e.reciprocal(out=t2[:, :], in_=t2[:, :]), r=["t2"], w=["t2"])
            P.op("dve", lambda e: e.tensor_tensor(out=t1[:, :], in0=t1[:, :], in1=t2[:, :], op=ALU.mult), r=["t1", "t2"], w=["t1"])
            P.op("dve", lambda e: e.tensor_tensor(out=t2[:, :], in0=t1[:, :], in1=a_t[:, :], op=ALU.mult), r=["t1", "a_t"], w=["t2"])
            P.op("dve", lambda e: e.tensor_scalar(out=t3[:, :], in0=a_t[:, :], scalar1=ppc("k_a", hp, 1), scalar2=omka[:, hp:hp + 1], op0=ALU.mult, op1=ALU.add), r=["a_t", "pp", "omka"], w=["t3"])
            P.op("dve", lambda e: e.tensor_tensor(out=k_t[:, :], in0=k_t[:, :], in1=t3[:, :], op=ALU.mult), r=["k_t", "t3"], w=["k_t"])
            P.op("dve", lambda e: e.tensor_tensor_scan(out=cl_t[:, :], data0=rmask[:, :], data1=ld_t[:, :], initial=0.0, op0=ALU.mult, op1=ALU.add), r=["rmask", "ld_t"], w=["cl_t"])
            P.op("act", lambda e: e.activation(out=t3[:, :], in_=cl_t[:, :], func=AF.Exp), r=["cl_t"], w=["t3"])
            P.op("dve", lambda e: e.tensor_tensor(out=RT[:, :], in0=r_t[:, :], in1=t3[:, :], op=ALU.mult), r=["r_t", "t3"], w=["RT"])
            P.op("dve", lambda e: e.tensor_tensor(out=ld_t[:, :], in0=cl_t[:, :], in1=ld_t[:, :], op=ALU.subtract), r=["cl_t", "ld_t"], w=["ld_t"])
            P.op("act", lambda e: e.activation(out=ld_t[:, :], in_=ld_t[:, :], func=AF.Exp), r=["ld_t"], w=["ld_t"])
            P.op("dve", lambda e: e.tensor_tensor(out=KT[:, :], in0=t1[:, :], in1=ld_t[:, :], op=ALU.mult), r=["t1", "ld_t"], w=["KT"])
            P.op("act", lambda e: e.activation(out=ld_t[:, :], in_=cl_t[:, :], func=AF.Exp, scale=-1.0), r=["cl_t"], w=["ld_t"])
            P.op("dve", lambda e: e.tensor_tensor(out=BT[:, :], in0=t2[:, :], in1=ld_t[:, :], op=ALU.mult), r=["t2", "ld_t"], w=["BT"])
            P.op("dve", lambda e: e.tensor_tensor(out=KK[:, :], in0=k_t[:, :], in1=ld_t[:, :], op=ALU.mult), r=["k_t", "ld_t"], w=["KK"])
            for c in range(NPC):
                P.op("act", lambda e, c=c: e.activation(out=ld_t[:, c * 128:(c + 1) * 128], in_=cl_t[:, c * 128:(c + 1) * 128], func=AF.Exp, bias=cl_t[:, c * 128 + 127:c * 128 + 128], scale=-1.0), r=["cl_t"], w=["ld_t"])
            if is_main:
                clv = cl_t[:, TP:TM].rearrange("p (s t) -> p s t", t=8)
                P.op("dve", lambda e: e.tensor_tensor(out=ld_t[:, TP:TM].rearrange("p (s t) -> p s t", t=8), in0=clv[:, :, 7:8].broadcast_to([128, NSEG, 8]), in1=clv, op=ALU.subtract), r=["cl_t"], w=["ld_t"])
                P.op("act", lambda e: e.activation(out=ld_t[:, TP:TM], in_=ld_t[:, TP:TM], func=AF.Exp), r=["ld_t"], w=["ld_t"])
            if is_main:
                P.op("dve", lambda e: e.scalar_tensor_tensor(out=t1[:, :], in0=r_t[:, :], scalar=ppc("r_k", hp, 1), in1=k_t[:, :], op0=ALU.mult, op1=ALU.mult), r=["r_t", "k_t", "pp"], w=["t1"])
            if is_main:
                P.dma("sp", Ss_f[:], swkvT[hp], w=["Ss_f"])
                P.op("act", lambda e: e.activation(out=Ss_b[:], in_=Ss_f[:], func=AF.Copy), r=["Ss_f"], w=["Ss_b"])
            P.op("act", lambda e: e.activation(out=S_b[:, :], in_=S_f[:, hp, :], func=AF.Copy), r=["S_f"], w=["S_b"])

            if SUB == 3:
                return
            def chunk_p12(c, slot):
                samp = (c == 8)
                cs = slice(c * 128, (c + 1) * 128)
                mg = maskg_s if samp else maskg_p
                ml = maskl_s if samp else maskl_p
                mgk, mlk = ("mgs", "mls") if samp else ("mgp", "mlp")
                tok3, tk = tok3s[slot], "tok3_%d" % slot
                bap6, bkey6 = bank(6)
                pT = bap6.bitcast(BF16)
                P.op("act", lambda e: e.activation(out=tmpc[:, 0, :], in_=v_t[:, cs], func=AF.Copy), r=["v_t"], w=["tmpc"])
                P.op("dve", lambda e: e.tensor_tensor(out=tmpc[:, 1, :], in0=t2[:, cs], in1=ld_t[:, cs], op=ALU.mult), r=["t2", "ld_t"], w=["tmpc"])
                P.op("dve", lambda e: e.tensor_tensor(out=tmpc[:, 2, :], in0=k_t[:, cs], in1=ld_t[:, cs], op=ALU.mult), r=["k_t", "ld_t"], w=["tmpc"])
                for i3 in range(3):
                    P.op("pe", lambda e, i3=i3: e.transpose(out=pT[:, i3 * 128:(i3 + 1) * 128], in_=tmpc[:, i3, :], identity=ident_b[:, :]), r=["tmpc", "ident_b"], w=[bkey6])
                P.op("act", lambda e: e.activation(out=tok3[:, :, :], in_=pT[:, 0:384].rearrange("p (a t) -> p a t", a=3), func=AF.Copy), r=[bkey6], w=[tk])
                for h in range(2):
                    hs = slice(h * 64, (h + 1) * 64)
                    bap, bkey = bank(0 + h)
                    G, gk = Gms[slot][h], "Gm%d_%d" % (slot, h)
                    L, lk_ = lvs[slot][:, h, :, :], "lv%d" % slot
                    for i4, (lh, lk, rh, rk) in enumerate([(BT, "BT", KT, "KT"), (BT, "BT", RT, "RT"), (KK, "KK", KT, "KT"), (KK, "KK", RT, "RT")]):
                        P.op("pe", lambda e, i4=i4, lh=lh, rh=rh, bap=bap, hs=hs: e.matmul(out=bap[:, i4 * 128:(i4 + 1) * 128], lhsT=lh[hs, cs], rhs=rh[hs, cs], start=True, stop=True), r=[lk, rk], w=[bkey])
                    P.op("dve", lambda e, G=G, bap=bap: e.tensor_tensor(out=G[:, :, :], in0=bap[:, :].rearrange("p (a t) -> p a t", a=4), in1=mg[:, :].rearrange("p (a t) -> p a t", a=4), op=ALU.mult), r=[bkey, mgk], w=[gk])
                    bap2, bkey2 = bank(2 if h == 0 else 7)
                    P.op("pe", lambda e, hs=hs, bap2=bap2: e.matmul(out=bap2[:, 0:128], lhsT=KT[hs, cs], rhs=BT[hs, cs], start=True, stop=True), r=["KT", "BT"], w=[bkey2])
                    P.op("dve", lambda e, L=L, bap2=bap2: e.tensor_tensor(out=L[:, 0, :], in0=bap2[:, 0:128], in1=ml[:, 0:128], op=ALU.mult), r=[bkey2, mlk], w=[lk_])
                    P.op("act", lambda e, L=L, G=G: e.activation(out=L[:, 1, :], in_=G[:, 0, :], func=AF.Copy), r=[gk], w=[lk_])
                    P.op("dve", lambda e, L=L, G=G: e.tensor_tensor(out=L[:, 2, :], in0=G[:, 0, :], in1=ident_b[:, :], op=ALU.add), r=[gk, "ident_b"], w=[lk_])

            def chunk_level(slot):
                L, lk_ = lvs[slot], "lv%d" % slot
                bq, bqk = bank([3, 4][slot % 2])
                bt, btk = bank([0, 1][slot % 2])
                for h in range(2):
                    Pm, Qm = L[:, h, 0, :], L[:, h, 1, :]
                    P.op("pe", lambda e, h=h, Pm=Pm, Qm=Qm: e.matmul(out=bq[:, (2 * h) * 128:(2 * h + 1) * 128], lhsT=Qm, rhs=Pm, start=True, stop=True), r=[lk_], w=[bqk])
                    P.op("pe", lambda e, h=h, Pm=Pm, Qm=Qm: e.matmul(out=bq[:, (2 * h + 1) * 128:(2 * h + 2) * 128], lhsT=Pm, rhs=Qm, start=True, stop=True), r=[lk_], w=[bqk])
                P.op("act", lambda e: e.activation(out=L[:, :, 0:2, :], in_=bq[:, :].rearrange("p (h a t) -> p h a t", h=2, a=2), func=AF.Copy), r=[bqk], w=[lk_])
                for h in range(2):
                    P.op("pe", lambda e, h=h: e.matmul(out=bt[:, h * 128:(h + 1) * 128], lhsT=L[:, h, 0, :], rhs=L[:, h, 2, :], start=True, stop=True), r=[lk_], w=[btk])
                P.op("dve", lambda e: e.tensor_tensor(out=L[:, :, 2, :], in0=L[:, :, 2, :], in1=bt[:, 0:256].rearrange("p (h t) -> p h t", h=2), op=ALU.add), r=[btk, lk_], w=[lk_])

            def chunk_chain(c, slot):
                samp = (c == 8)
                cs = slice(c * 128, (c + 1) * 128)
                tok3, tk = tok3s[slot], "tok3_%d" % slot
                Gm = Gms[slot]
                gks = ["Gm%d_%d" % (slot, h) for h in range(2)]
                TT = [lvs[slot][:, h, 2, :] for h in range(2)]
                TTk = ["lv%d" % slot for h in range(2)]
                if samp:
                    P.op("dve", lambda e: e.tensor_tensor(out=BGseg[:, :, :], in0=tok3[:, 1:2, :].broadcast_to([128, NSEG, 128]), in1=segt[:, :].unsqueeze(2).broadcast_to([128, NSEG, 128]), op=ALU.mult), r=[tk, "segt"], w=["cl_t"])
                    P.op("dve", lambda e: e.tensor_tensor(out=KGseg[:, :, :], in0=tok3[:, 2:3, :].broadcast_to([128, NSEG, 128]), in1=segt[:, :].unsqueeze(2).broadcast_to([128, NSEG, 128]), op=ALU.mult), r=[tk, "segt"], w=["KGseg"])
                    P.op("dve", lambda e: e.tensor_tensor(out=KTseg[:, :, :], in0=KT[:, cs].unsqueeze(1).broadcast_to([128, NSEG, 128]), in1=segf[:, :, :], op=ALU.mult), r=["KT", "segf"], w=["r_t"])
                bap5, bkey5 = bank(5)
                for h in range(2):
                    hs = slice(h * 64, (h + 1) * 64)
                    o = bap5[:, h * 64:(h + 1) * 64]
                    if samp:
                        for sg_ in range(NSEG):
                            P.op("pe", lambda e, sg_=sg_, hs=hs, o=o: e.matmul(out=o, lhsT=KTseg[hs, sg_, :], rhs=Ss_b[hs, sg_, :], start=(sg_ == 0), stop=False), r=["r_t", "Ss_b"], w=[bkey5])
                    else:
                        P.op("pe", lambda e, hs=hs, o=o: e.matmul(out=o, lhsT=KT[hs, cs], rhs=S_b[hs, :], start=True, stop=False), r=["KT", "S_b"], w=[bkey5])
                    P.op("pe", lambda e, h=h, hs=hs, o=o: e.matmul(out=o, lhsT=Gm[h][:, 2, :], rhs=tok3[:, 0, hs], start=False, stop=True), r=[gks[h], tk], w=[bkey5])
                P.op("act", lambda e: e.activation(out=RHSb[:, :], in_=bap5[:, 0:128], func=AF.Copy, scale=-1.0), r=[bkey5], w=["RHSb"])
                for h in range(2):
                    hs = slice(h * 64, (h + 1) * 64)
                    P.op("pe", lambda e, h=h, hs=hs: e.matmul(out=bap5[:, 128 + h * 64:128 + (h + 1) * 64], lhsT=TT[h], rhs=RHSb[:, hs], start=True, stop=True), r=[TTk[h], "RHSb"], w=[bkey5])
                P.op("dve", lambda e: e.tensor_copy(out=Ub[:, :], in_=bap5[:, 128:256]), r=[bkey5], w=["Ub"])
                if is_main:
                    if samp:
                        P.op("dve", lambda e: e.tensor_tensor(out=RTseg[:, :, :], in0=RT[:, cs].unsqueeze(1).broadcast_to([128, NSEG, 128]), in1=segf[:, :, :], op=ALU.mult), r=["RT", "segf"], w=["r_t"])
                    yb, ykey = bank(2)
                    yo = yb[:, 256:384]
                    for h in range(2):
                        hs = slice(h * 64, (h + 1) * 64)
                        o = yo[hs, :]
                        P.op("pe", lambda e, h=h, hs=hs, o=o: e.matmul(out=o, lhsT=Ub[:, hs], rhs=Gm[h][:, 1, :], start=True, stop=False), r=["Ub", gks[h]], w=[ykey])
                        P.op("pe", lambda e, h=h, hs=hs, o=o: e.matmul(out=o, lhsT=tok3[:, 0, hs], rhs=Gm[h][:, 3, :], start=False, stop=False), r=[tk, gks[h]], w=[ykey])
                        if samp:
                            for sg_ in range(NSEG):
                                P.op("pe", lambda e, sg_=sg_, hs=hs, o=o: e.matmul(out=o, lhsT=Ss_b[hs, sg_, :], rhs=RTseg[hs, sg_, :], start=False, stop=(sg_ == NSEG - 1)), r=["Ss_b", "r_t"], w=[ykey])
                        else:
                            P.op("pe", lambda e, hs=hs, o=o: e.matmul(out=o, lhsT=S_b[hs, :], rhs=RT[hs, cs], start=False, stop=True), r=["S_b", "RT"], w=[ykey])
                    P.op("act", lambda e: e.activation(out=yT[:, cs], in_=yo, func=AF.Copy), r=[ykey], w=["a_t"])
                if samp:
                    sbank = [bank(6), bank(7)]
                    for h in range(2):
                        hs = slice(h * 64, (h + 1) * 64)
                        for sg_ in range(NSEG):
                            bap, bkey = sbank[sg_ // 8]
                            o = bap[hs, (sg_ % 8) * 64:(sg_ % 8 + 1) * 64]
                            P.op("pe", lambda e, sg_=sg_, hs=hs, o=o: e.matmul(out=o, lhsT=BGseg[:, sg_, hs], rhs=Ub[:, hs], start=True, stop=False), r=["cl_t", "Ub"], w=[bkey])
                            P.op("pe", lambda e, sg_=sg_, hs=hs, o=o: e.matmul(out=o, lhsT=KGseg[:, sg_, hs], rhs=tok3[:, 0, hs], start=False, stop=True), r=["KGseg", tk], w=[bkey])
                    gam = t3[:, TP:TM].rearrange("p (s t) -> p s t", t=8)[:, :, 7:8]
                    P.op("dve", lambda e: e.tensor_tensor(out=Ss_n[:, :, :], in0=Ss_f[:, :, :], in1=gam.broadcast_to([128, NSEG, 64]), op=ALU.mult), r=["Ss_f", "t3"], w=["Ss_f"])
                    for half in range(2):
                        bap, bkey = sbank[half]
                        P.op("dve", lambda e, half=half, bap=bap: e.tensor_tensor(out=Ss_n[:, half * 8:(half + 1) * 8, :], in0=Ss_n[:, half * 8:(half + 1) * 8, :], in1=bap[:, :].rearrange("p (s i) -> p s i", s=8), op=ALU.add), r=["Ss_f", bkey], w=["Ss_f"])
                    P.dma("sp", wkvs_out[hp], Ss_n[:], r=["Ss_f"], sem="d_wkvs")
                else:
                    bap, bkey = bank(7)
                    for h in range(2):
                        hs = slice(h * 64, (h + 1) * 64)
                        o = bap[hs, 0:64]
                        P.op("pe", lambda e, hs=hs, o=o: e.matmul(out=o, lhsT=tok3[:, 1, hs], rhs=Ub[:, hs], start=True, stop=False), r=[tk, "Ub"], w=[bkey])
                        P.op("pe", lambda e, hs=hs, o=o: e.matmul(out=o, lhsT=tok3[:, 2, hs], rhs=tok3[:, 0, hs], start=False, stop=True), r=[tk], w=[bkey])
                    P.op("dve", lambda e, c=c, bap=bap: e.scalar_tensor_tensor(out=S_f[:, hp, :], in0=S_f[:, hp, :], scalar=t3[:, c * 128 + 127:c * 128 + 128], in1=bap[:, 0:64], op0=ALU.mult, op1=ALU.add), r=["S_f", "t3", bkey], w=["S_f"])
                    P.op("act", lambda e: e.activation(out=S_b[:, :], in_=S_f[:, hp, :], func=AF.Copy), r=["S_f"], w=["S_b"])

            groups = [[0, 1, 2, 3, 4], [5, 6, 7, 8]] if is_main else [[0, 1, 2, 3, 4, 5, 6, 7]]
            for grp in groups:
                for slot, c in enumerate(grp):
                    chunk_p12(c, slot)
                for m in range(1, 7):
                    for slot, c in enumerate(grp):
                        chunk_level(slot)
                for slot, c in enumerate(grp):
                    chunk_chain(c, slot)
            if SUB == 4:
                return
            if is_main:
                for j in range(nbt):
                    c0, c1 = j * 512, min(TY, j * 512 + 512)
                    bap, bkey = bank(0 + j)
                    P.op("pe", lambda e, c0=c0, c1=c1, bap=bap: e.matmul(out=bap[:, 0:c1 - c0], lhsT=blks[:, :], rhs=yT[:, c0:c1], start=True, stop=True), r=["blks", "a_t"], w=[bkey])
                P.op("dve", lambda e: e.tensor_tensor(out=yT[:, :], in0=yT[:, :], in1=zpa, op=ALU.subtract), r=["a_t"] + zkeysa, w=["a_t"])
                P.op("act", lambda e: e.activation(out=t2[:, :], in_=yT[:, :], func=AF.Square), r=["a_t"], w=["t2"])
                for j in range(nbt):
                    c0, c1 = j * 512, min(TY, j * 512 + 512)
                    bap, bkey = bank(3 + j)
                    P.op("pe", lambda e, c0=c0, c1=c1, bap=bap: e.matmul(out=bap[:, 0:c1 - c0], lhsT=blks[:, :], rhs=t2[:, c0:c1], start=True, stop=True), r=["blks", "t2"], w=[bkey])
                P.op("act", lambda e: e.activation(out=t2[:, :], in_=zp, func=AF.Sqrt, bias=EPS_GN(), scale=1.0), r=zkeys + ["pp"], w=["t2"])
                P.op("dve", lambda e: e.reciprocal(out=t2[:, :], in_=t2[:, :]), r=["t2"], w=["t2"])
                P.op("dve", lambda e: e.tensor_tensor(out=yT[:, :], in0=yT[:, :], in1=t2[:, :], op=ALU.mult), r=["a_t", "t2"], w=["a_t"])
                P.op("dve", lambda e: e.tensor_scalar(out=yT[:, :], in0=yT[:, :], scalar1=ppc("lnx_w", hp, 1), scalar2=ppc("lnx_b", hp, 1), op0=ALU.mult, op1=ALU.add), r=["a_t", "pp"], w=["a_t"])
                for j in range(nbt):
                    c0, c1 = j * 512, min(TY, j * 512 + 512)
                    bap, bkey = bank(0 + j)
                    P.op("pe", lambda e, c0=c0, c1=c1, bap=bap: e.matmul(out=bap[:, 0:c1 - c0], lhsT=blk[:, :], rhs=t1[:, c0:c1], start=True, stop=True), r=["blk", "t1"], w=[bkey])
                P.op("dve", lambda e: e.tensor_tensor(out=t2[:, :], in0=v_t[:, :], in1=zpa, op=ALU.mult), r=["v_t"] + zkeysa, w=["t2"])
                P.op("dve", lambda e: e.tensor_tensor(out=yT[:, :], in0=yT[:, :], in1=t2[:, :], op=ALU.add), r=["a_t", "t2"], w=["a_t"])
                for j in range(nbt):
                    c0, c1 = j * 512, min(TY, j * 512 + 512)
                    bap, bkey = bank(3 + j)
                    for kc in range(2):
                        P.op("pe", lambda e, c0=c0, c1=c1, bap=bap, kc=kc: e.matmul(out=bap[:, 0:c1 - c0], lhsT=g2b[:, kc, :], rhs=sgT[:, kc, c0:c1], start=(kc == 0), stop=(kc == 1)), r=[g2k, "sgT"], w=[bkey])
                P.op("dve", lambda e: e.tensor_tensor(out=ob[:, :], in0=yT[:, :], in1=zp, op=ALU.mult), r=["a_t"] + zkeys, w=["KK"])
                P.dma("sp", catT[f0:f0 + 128, :], ob[:, :], r=["KK"], sem="d_cat")
        if is_main:
            P.dma("sp", wkvp_out.rearrange("h p i -> p h i"), S_f[:, :, :], r=["S_f"], sem="d_wkvp")

    def conv_phase(st, xT, xkey, E):
        nonlocal wslots
        wslots = [sb(st, "cws%d" % i, [128, 32, 128], BF16) for i in range(2)]
        wstate["i"] = 0
        scT = sb(st, "scT", [128, 16, NSEG, 30])
        st0 = ExitStack()
        sct = [sb(st0, "sct%d" % i, [120, DR]) for i in range(2)]
        for rt in range(4):
            s = rt % 2
            P.dma("sp", sct[s][:, :], sconv[rt * 120:(rt + 1) * 120, :], w=["sct%d" % s])
            for cc in range(16):
                bap, bkey = bank(6 + cc % 2)
                P.op("pe", lambda e, s=s, cc=cc, bap=bap: e.transpose(out=bap[:, 0:120], in_=sct[s][0:120, cc * 128:(cc + 1) * 128], identity=ident_f[0:120, 0:120]), r=["sct%d" % s, "ident_f"], w=[bkey])
                P.op("act" if cc % 2 == 0 else "dve",
                     (lambda e, cc=cc, rt=rt, bap=bap: e.activation(out=scT[:, cc, rt * 4:(rt + 1) * 4, :], in_=bap[:, 0:120].rearrange("p (s t) -> p s t", t=30), func=AF.Copy)) if cc % 2 == 0 else
                     (lambda e, cc=cc, rt=rt, bap=bap: e.tensor_copy(out=scT[:, cc, rt * 4:(rt + 1) * 4, :], in_=bap[:, 0:120].rearrange("p (s t) -> p s t", t=30))),
                     r=[bkey], w=["scT"])
        P.barrier()
        st0.close()
        P.dma("sp", convs_out[:, 0:22, :], sconv.rearrange("(s t) c -> s t c", t=30)[:, 8:30, :], sem="d_cvcp")
        utok = sb(st, "utok", [128, DR])
        ulast = sb(st, "ulast", [30, DR])
        sum1 = sb(st, "sum1", [128, TM])
        sum2 = sb(st, "sum2", [128, TM])
        onesc = sb(st, "onesc", [128, 128])
        P.dma("sp", onesc[:], onesc_d, w=["onesc"])
        sl = ExitStack()
        uXs = [sb(sl, "uX%d" % i, [128, E]) for i in range(2)]
        sgs = [sb(sl, "sg%d" % i, [128, E]) for i in range(2)]
        exts = [sb(sl, "ext%d" % i, [128, NSEG, 38]) for i in range(2)]
        accs_ = [sb(sl, "acc%d" % i, [128, TM]) for i in range(2)]
        sqcs = [sb(sl, "sqc%d" % i, [128, TM]) for i in range(2)]
        usm = sb(sl, "usm", [128, 128])
        nbt = 3
        for cc in range(16):
            pb = cc % 2
            uX, sg, ext, acc, sqc = uXs[pb], sgs[pb], exts[pb], accs_[pb], sqcs[pb]
            kuX, ksg, kext, kacc, ksqc = "uX%d" % pb, "sg%d" % pb, "ext%d" % pb, "acc%d" % pb, "sqc%d" % pb
            wv, wvk = load_wcol(RW + cc * 128, 128)
            gemm_fm(wv, wvk, 128, xT, xkey, E, 0)
            wg, wgk = load_wcol(RW + DR + cc * 128, 128)
            gemm_fm(wg, wgk, 128, xT, xkey, E, 3)
            zv, zvk = zps(0, 128, E)
            zg, zgk = zps(3, 128, E)
            P.op("act", lambda e: e.activation(out=sg[:, :], in_=zg, func=AF.Sigmoid), r=zgk, w=[ksg])
            P.op("dve", lambda e: e.tensor_tensor(out=uX[:, :], in0=zv, in1=sg[:, :], op=ALU.mult), r=zvk + [ksg], w=[kuX])
            cw = lambda k, cc=cc: pp[:, PP["conv_w"][0] + cc * 31 + k:PP["conv_w"][0] + cc * 31 + k + 1]
            P.op("dve", lambda e: e.tensor_scalar(out=acc[:, 0:TP], in0=uX[:, 2:2 + TP], scalar1=cw(0), scalar2=ppc("conv_b", cc, 1), op0=ALU.mult, op1=ALU.add), r=[kuX, "pp"], w=[kacc])
            for k in range(1, 31):
                P.op("dve", lambda e, k=k: e.scalar_tensor_tensor(out=acc[:, 0:TP], in0=uX[:, 2 + k:2 + k + TP], scalar=cw(k), in1=acc[:, 0:TP], op0=ALU.mult, op1=ALU.add), r=[kuX, "pp", kacc], w=[kacc])
            P.op("act", lambda e: e.activation(out=ext[:, :, 0:30], in_=scT[:, cc, :, :], func=AF.Copy), r=["scT"], w=[kext])
            P.op("act", lambda e: e.activation(out=ext[:, :, 30:38], in_=uX[:, 32 + TP:E].rearrange("p (s t) -> p s t", t=9)[:, :, 1:9], func=AF.Copy), r=[kuX], w=[kext])
            accs = acc[:, TP:TM].rearrange("p (s t) -> p s t", t=8)
            P.op("dve", lambda e: e.tensor_scalar(out=accs, in0=ext[:, :, 0:8], scalar1=cw(0), scalar2=ppc("conv_b", cc, 1), op0=ALU.mult, op1=ALU.add), r=[kext, "pp"], w=[kacc])
            for k in range(1, 31):
                P.op("dve", lambda e, k=k: e.scalar_tensor_tensor(out=accs, in0=ext[:, :, k:k + 8], scalar=cw(k), in1=accs, op0=ALU.mult, op1=ALU.add), r=[kext, "pp", kacc], w=[kacc])
            P.dma("sp", cTd[cc * 128:(cc + 1) * 128, :], acc[:, :], r=[kacc], sem="d_cTd")
            P.op("act", lambda e: e.activation(out=sqc[:, :], in_=acc[:, :], func=AF.Square), r=[kacc], w=[ksqc])
            for (srcq, skey, dstq, dkey, b0) in [(acc, kacc, sum1, "sum1", 0), (sqc, ksqc, sum2, "sum2", 3)]:
                for j in range(nbt):
                    c0, c1 = j * 512, min(TM, j * 512 + 512)
                    bap, bkey = bank(b0 + j)
                    P.op("pe", lambda e, c0=c0, c1=c1, bap=bap, srcq=srcq: e.matmul(out=bap[:, 0:c1 - c0], lhsT=onesc[:, :], rhs=srcq[:, c0:c1], start=True, stop=True), r=["onesc", skey], w=[bkey])
                zz = (zA if b0 == 0 else zB)[:, 0:TM]
                zzk = ["ps%d" % (b0 + j) for j in range(nbt)]
                if cc == 0:
                    P.op("act", lambda e, zz=zz, dstq=dstq: e.activation(out=dstq[:, :], in_=zz, func=AF.Copy), r=zzk, w=[dkey])
                else:
                    P.op("dve", lambda e, zz=zz, dstq=dstq: e.tensor_tensor(out=dstq[:, :], in0=dstq[:, :], in1=zz, op=ALU.add), r=zzk + [dkey], w=[dkey])
            P.op("act", lambda e: e.activation(out=usm[:, :].rearrange("p (s t) -> p s t", t=8), in_=ext[:, :, 30:38], func=AF.Copy), r=[kext], w=["usm"])
            bap, bkey = bank(6)
            P.op("pe", lambda e, bap=bap: e.transpose(out=bap[:, 0:128], in_=usm[:, :], identity=ident_f[:, :]), r=["usm", "ident_f"], w=[bkey])
            P.op("act", lambda e, cc=cc, bap=bap: e.activation(out=utok[:, cc * 128:(cc + 1) * 128], in_=bap[:, 0:128], func=AF.Copy), r=[bkey], w=["utok"])
            bap7, bkey7 = bank(7)
            P.op("pe", lambda e, bap7=bap7: e.transpose(out=bap7[0:30, 0:128], in_=uX[:, 32 + TP - 30:32 + TP], identity=ident_f[:, :]), r=[kuX, "ident_f"], w=[bkey7])
            P.op("act", lambda e, cc=cc, bap7=bap7: e.activation(out=ulast[0:30, cc * 128:(cc + 1) * 128], in_=bap7[0:30, 0:128], func=AF.Copy), r=[bkey7], w=["ulast"])
        for s_ in range(NSEG):
            P.dma("sp", convs_out[s_, 22:30, :], utok[s_ * 8:(s_ + 1) * 8, :], r=["utok"], sem="d_cvs%d" % (s_ % 4))
        P.dma("sp", convp_out[:, :], ulast[:, :], r=["ulast"], sem="d_cvp")
        P.barrier()
        sl.close()
        sqc = sb(st, "sqcf", [128, TM])
        P.op("dve", lambda e: e.tensor_tensor(out=sqc[:, :], in0=sum1[:, :], in1=sum1[:, :], op=ALU.mult), r=["sum1"], w=["sqc"])
        P.op("dve", lambda e: e.tensor_tensor(out=sum2[:, :], in0=sum2[:, :], in1=sqc[:, :], op=ALU.subtract), r=["sum2", "sqc"], w=["sum2"])
        P.op("act", lambda e: e.activation(out=sum2[:, :], in_=sum2[:, :], func=AF.Sqrt, bias=EPS_LN(), scale=1.0), r=["sum2", "pp"], w=["sum2"])
        P.op("dve", lambda e: e.reciprocal(out=sum2[:, :], in_=sum2[:, :]), r=["sum2"], w=["sum2"])
        cin = [sb(st, "cin%d" % i, [128, TM]) for i in range(2)]
        cob = [sb(st, "cob%d" % i, [128, TM], BF16) for i in range(2)]
        for cc in range(16):
            s = cc % 2
            wait_dram("d_cTd")
            P.dma("sp", cin[s][:, :], cTd[cc * 128:(cc + 1) * 128, :], w=["cin%d" % s])
            P.op("dve", lambda e, s=s: e.tensor_tensor(out=cin[s][:, :], in0=cin[s][:, :], in1=sum1[:, :], op=ALU.subtract), r=["cin%d" % s, "sum1"], w=["cin%d" % s])
            P.op("dve", lambda e, s=s: e.tensor_tensor(out=cin[s][:, :], in0=cin[s][:, :], in1=sum2[:, :], op=ALU.mult), r=["cin%d" % s, "sum2"], w=["cin%d" % s])
            P.op("act", lambda e, s=s, cc=cc: e.activation(out=cob[s][:, :], in_=cin[s][:, :], func=AF.Silu, bias=ppc("cln_b", cc, 1), scale=ppc("cln_w", cc, 1)), r=["cin%d" % s, "pp"], w=["cob%d" % s])
            P.dma("sp", catT[DR + cc * 128:DR + (cc + 1) * 128, :], cob[s][:, :], r=["cob%d" % s], sem="d_cat")

    def load_actT(st, name, src_d, nk, tok0, ntok):
        t = sb(st, name, [128, nk, ntok], BF16)
        for k0 in range(0, nk, 16):
            k1 = min(nk, k0 + 16)
            P.dma("sp", t[:, k0:k1, :], src_d[k0 * 128:k1 * 128, tok0:tok0 + ntok].rearrange("(k p) n -> p k n", p=128), w=[name], sem="d_%s_%d" % (name, (k0 // 16) % 4))
        return t

    def wait_dram(semname, q="sp"):
        if semname in P.n2p:
            pn = P.n2p[semname]
            if P.dcnt[pn] > 0:
                for eng in ("sp", "pool"):
                    P._wait(eng, ("d", pn, P.dcnt[pn]))

    def out_phase():
        with ExitStack() as st:
            wait_dram("d_cat")
            aT = load_actT(st, "caT", catT, 32, 0, TM)
            ws = [sb(st, "wo%d" % i, [128, 32, 512], BF16) for i in range(2)]
            xin = [sb(st, "oxin%d" % i, [128, 512]) for i in range(3)]
            ho = [sb(st, "oho%d" % i, [128, 512]) for i in range(3)]
            sqj = sb(st, "osq", [128, 512])
            ssp = sb(st, "ossp", [128, 9, 8])
            it = 0
            for nb in range(8):
                s = nb % 2
                P.dma("pool", ws[s][:, :, :], w_out[:, nb * 512:(nb + 1) * 512].rearrange("(k p) n -> p k n", p=128), w=["wo%d" % s])
                for tt in range(9):
                    bap, bkey = bank(it % 6)
                    xs = it % 3
                    it += 1
                    P.dma("sp", xin[xs][:, :], x_main[tt * 128:(tt + 1) * 128, nb * 512:(nb + 1) * 512], w=["oxin%d" % xs])
                    for k in range(32):
                        P.op("pe", lambda e, k=k, s=s, tt=tt, bap=bap: e.matmul(out=bap[:, :], lhsT=aT[:, k, tt * 128:(tt + 1) * 128], rhs=ws[s][:, k, :], start=(k == 0), stop=(k == 31)), r=["caT", "wo%d" % s], w=[bkey])
                    P.op("dve", lambda e, xs=xs, bap=bap: e.tensor_tensor(out=ho[xs][:, :], in0=bap[:, :], in1=xin[xs][:, :], op=ALU.add), r=[bkey, "oxin%d" % xs], w=["oho%d" % xs])
                    P.op("act", lambda e, xs=xs: e.activation(out=sqj[:, :], in_=ho[xs][:, :], func=AF.Square), r=["oho%d" % xs], w=["osq"])
                    P.op("dve", lambda e, tt=tt, nb=nb: e.reduce_sum(out=ssp[:, tt, nb:nb + 1], in_=sqj[:, :], axis=AX.X), r=["osq"], w=["ossp"])
                    P.dma("sp", h1d[tt * 128:(tt + 1) * 128, nb * 512:(nb + 1) * 512], ho[xs][:, :], r=["oho%d" % xs], sem="d_h1_%d" % xs)
            P.op("dve", lambda e: e.reduce_sum(out=ss_all[:, 0:9], in_=ssp[:, :, :], axis=AX.X), r=["ossp"], w=["ss_all"])
            P.barrier()

    ss_all = sb(es, "ss_all", [128, 32])

    def norm_from_dram(st, src_d, sems, gcol, ss_col0, xT, xkey, tag):
        for sname in sems:
            wait_dram(sname)
        gb = sb(st, tag + "_gb", [128, D])
        P.dma("sp", gb[:], gvec[gcol].partition_broadcast(128), w=[tag + "_gb"])
        xt = [sb(st, tag + "_xt%d" % i, [128, D]) for i in range(2)]
        xnb = [sb(st, tag + "_xnb%d" % i, [128, D], BF16) for i in range(2)]
        rs = sb(st, tag + "_rs", [128, 9])
        P.op("act", lambda e: e.activation(out=rs[:, :], in_=ss_all[:, ss_col0:ss_col0 + 9], func=AF.Sqrt, bias=EPS_RMS(), scale=1.0 / D), r=["ss_all", "pp"], w=[tag + "_rs"])
        P.op("dve", lambda e: e.reciprocal(out=rs[:, :], in_=rs[:, :]), r=[tag + "_rs"], w=[tag + "_rs"])
        for tt in range(9):
            s = tt % 2
            P.dma("sp", xt[s][:, :], src_d[tt * 128:(tt + 1) * 128, :], w=[tag + "_xt%d" % s])
            P.op("dve", lambda e, s=s, tt=tt: e.scalar_tensor_tensor(out=xnb[s][:, :], in0=xt[s][:, :], scalar=rs[:, tt:tt + 1], in1=gb[:, :], op0=ALU.mult, op1=ALU.mult), r=[tag + "_xt%d" % s, tag + "_rs", tag + "_gb"], w=[tag + "_xnb%d" % s])
            transpose_tile(xnb[s], tag + "_xnb%d" % s, 128, lambda kk, tt=tt: xT[:, kk * 8:(kk + 1) * 8, tt * 128:(tt + 1) * 128], xkey)

    def ffn_up_phase():
        with ExitStack() as st:
            xT = sb(st, "fxT", [128, 32, TM], BF16)
            with ExitStack() as sa:
                norm_from_dram(sa, h1d, ["d_h1_0", "d_h1_1", "d_h1_2"], 1, 0, xT, "fxT", "fn")
                P.barrier()
            wsl = [sb(st, "fw%d" % i, [128, 32, 128], BF16) for i in range(6)]
            sgl = sb(st, "fsgl", [128, TM])
            ao = [sb(st, "fao%d" % i, [128, TM], BF16) for i in range(2)]
            li = 0
            for fc in range(NFC):
                sg_i, su_i = (li % 6), ((li + 1) % 6)
                li += 2
                P.dma("pool", wsl[sg_i][:, :, :], w_gu[:, fc * 128:(fc + 1) * 128].rearrange("(k p) n -> p k n", p=128), w=["fw%d" % sg_i])
                P.dma("pool", wsl[su_i][:, :, :], w_gu[:, DFF + fc * 128:DFF + (fc + 1) * 128].rearrange("(k p) n -> p k n", p=128), w=["fw%d" % su_i])
                gemm_fm(wsl[sg_i], "fw%d" % sg_i, 128, xT, "fxT", TM, 0)
                gemm_fm(wsl[su_i], "fw%d" % su_i, 128, xT, "fxT", TM, 3)
                s = fc % 2
                P.op("act", lambda e: e.activation(out=sgl[:, :], in_=zA[:, 0:TM], func=AF.Silu), r=["ps0", "ps1", "ps2"], w=["fsgl"])
                P.op("dve", lambda e, s=s: e.tensor_tensor(out=ao[s][:, :], in0=zB[:, 0:TM], in1=sgl[:, :], op=ALU.mult), r=["ps3", "ps4", "ps5", "fsgl"], w=["fao%d" % s])
                P.dma("sp", actT[fc * 128:(fc + 1) * 128, :], ao[s][:, :], r=["fao%d" % s], sem="d_act%d" % s)
            P.barrier()

    def tm_gemm_phase(tag, aT_d, a_sems, nk, Wd, res_d, res_sems, out_d, out_semtag, ss_col0, groups, wcols=512, nslots=4):
        for sname in a_sems + res_sems:
            wait_dram(sname)
        for gi, (t0, nt) in enumerate(groups):
            with ExitStack() as st:
                aT = load_actT(st, "%saT%d" % (tag, gi), aT_d, nk, t0 * 128, nt * 128)
                akey = "%saT%d" % (tag, gi)
                nhalf = 2
                kh = (nk + 1) // 2
                ws = [sb(st, "%sw%d_%d" % (tag, gi, i), [128, kh, wcols], BF16) for i in range(nslots)]
                xin = [sb(st, "%sxin%d_%d" % (tag, gi, i), [128, wcols]) for i in range(3)]
                ho = [sb(st, "%sho%d_%d" % (tag, gi, i), [128, wcols]) for i in range(3)]
                sqj = sb(st, "%ssq%d" % (tag, gi), [128, wcols])
                nblk = D // wcols
                ssp = sb(st, "%sssp%d" % (tag, gi), [128, nt, nblk])
                it = 0
                li = 0
                for nb in range(nblk):
                    slots = []
                    for hf in range(2):
                        si = li % nslots
                        li += 1
                        k0, k1 = hf * kh, min(nk, (hf + 1) * kh)
                        P.dma("pool", ws[si][:, 0:k1 - k0, :], Wd[k0 * 128:k1 * 128, nb * wcols:(nb + 1) * wcols].rearrange("(k p) n -> p k n", p=128), w=["%sw%d_%d" % (tag, gi, si)])
                        slots.append((si, k0, k1))
                    for tl in range(nt):
                        tt = t0 + tl
                        bap, bkey = bank(it % 6)
                        xs = it % 3
                        it += 1
                        P.dma("sp", xin[xs][:, :], res_d[tt * 128:(tt + 1) * 128, nb * wcols:(nb + 1) * wcols], w=["%sxin%d_%d" % (tag, gi, xs)])
                        for (si, k0, k1) in slots:
                            for k in range(k0, k1):
                                P.op("pe", lambda e, k=k, k0=k0, si=si, tl=tl, bap=bap: e.matmul(out=bap[:, 0:wcols], lhsT=aT[:, k, tl * 128:(tl + 1) * 128], rhs=ws[si][:, k - k0, :], start=(k == 0), stop=(k == nk - 1)), r=[akey, "%sw%d_%d" % (tag, gi, si)], w=[bkey])
                        P.op("dve", lambda e, xs=xs, bap=bap: e.tensor_tensor(out=ho[xs][:, :], in0=bap[:, 0:wcols], in1=xin[xs][:, :], op=ALU.add), r=[bkey, "%sxin%d_%d" % (tag, gi, xs)], w=["%sho%d_%d" % (tag, gi, xs)])
                        P.op("act", lambda e, xs=xs: e.activation(out=sqj[:, :], in_=ho[xs][:, :], func=AF.Square), r=["%sho%d_%d" % (tag, gi, xs)], w=["%ssq%d" % (tag, gi)])
                        P.op("dve", lambda e, tl=tl, nb=nb: e.reduce_sum(out=ssp[:, tl, nb:nb + 1], in_=sqj[:, :], axis=AX.X), r=["%ssq%d" % (tag, gi)], w=["%sssp%d" % (tag, gi)])
                        P.dma("sp", out_d[tt * 128:(tt + 1) * 128, nb * wcols:(nb + 1) * wcols], ho[xs][:, :], r=["%sho%d_%d" % (tag, gi, xs)], sem="%s_%d" % (out_semtag, xs))
                P.op("dve", lambda e: e.reduce_sum(out=ss_all[:, ss_col0 + t0:ss_col0 + t0 + nt], in_=ssp[:, :, :], axis=AX.X), r=["%sssp%d" % (tag, gi)], w=["ss_all"])
                P.barrier()

    def ple_phase():
        with ExitStack() as st:
            xT = sb(st, "plxT", [128, 32, TM], BF16)
            peT = sb(st, "peT", [128, 2, TM], BF16)
            with ExitStack() as sa:
                norm_from_dram(sa, h2d, ["d_h2_0", "d_h2_1", "d_h2_2"], 2, 9, xT, "plxT", "pn")
                pin = [sb(sa, "pin%d" % i, [128, PLE]) for i in range(2)]
                pinb = [sb(sa, "pinb%d" % i, [128, PLE], BF16) for i in range(2)]
                for tt in range(9):
                    s = tt % 2
                    P.dma("sp", pin[s][:, :], p_main[tt * 128:(tt + 1) * 128, :], w=["pin%d" % s])
                    P.op("dve", lambda e, s=s: e.tensor_copy(out=pinb[s][:, :], in_=pin[s][:, :]), r=["pin%d" % s], w=["pinb%d" % s])
                    bap, bkey = bank(6 + tt % 2)
                    pT = bap.bitcast(BF16)
                    for q in range(2):
                        P.op("pe", lambda e, s=s, q=q, pT=pT: e.transpose(out=pT[:, q * 128:(q + 1) * 128], in_=pinb[s][:, q * 128:(q + 1) * 128], identity=ident_b[:, :]), r=["pinb%d" % s, "ident_b"], w=[bkey])
                    P.op("act", lambda e, tt=tt, pT=pT: e.activation(out=peT[:, :, tt * 128:(tt + 1) * 128], in_=pT[:, 0:256].rearrange("p (q t) -> p q t", q=2), func=AF.Copy), r=[bkey], w=["peT"])
                P.barrier()
            ws = [sb(st, "pw%d" % i, [128, 32, 512], BF16) for i in range(2)]
            wp = [sb(st, "pwp%d" % i, [128, 2, 512], BF16) for i in range(2)]
            xin = [sb(st, "pxin%d" % i, [128, 512]) for i in range(3)]
            ho = [sb(st, "pho%d" % i, [128, 512]) for i in range(3)]
            pgs = sb(st, "pgs", [128, 512])
            sqj = sb(st, "psq", [128, 512])
            ssp = sb(st, "pssp", [128, 9, 8])
            it = 0
            for nb in range(8):
                s = nb % 2
                P.dma("pool", ws[s][:, :, :], w_pg[:, nb * 512:(nb + 1) * 512].rearrange("(k p) n -> p k n", p=128), w=["pw%d" % s])
                P.dma("pool", wp[s][:, :, :], w_pp[:, nb * 512:(nb + 1) * 512].rearrange("(k p) n -> p k n", p=128), w=["pwp%d" % s])
                for tt in range(9):
                    b1, k1 = bank((2 * it) % 6)
                    b2, k2 = bank((2 * it + 1) % 6)
                    xs = it % 3
                    it += 1
                    P.dma("sp", xin[xs][:, :], h2d[tt * 128:(tt + 1) * 128, nb * 512:(nb + 1) * 512], w=["pxin%d" % xs])
                    for k in range(32):
                        P.op("pe", lambda e, k=k, s=s, tt=tt, b1=b1: e.matmul(out=b1[:, :], lhsT=xT[:, k, tt * 128:(tt + 1) * 128], rhs=ws[s][:, k, :], start=(k == 0), stop=(k == 31)), r=["plxT", "pw%d" % s], w=[k1])
                    for k in range(2):
                        P.op("pe", lambda e, k=k, s=s, tt=tt, b2=b2: e.matmul(out=b2[:, :], lhsT=peT[:, k, tt * 128:(tt + 1) * 128], rhs=wp[s][:, k, :], start=(k == 0), stop=(k == 1)), r=["peT", "pwp%d" % s], w=[k2])
                    P.op("act", lambda e, b1=b1: e.activation(out=pgs[:, :], in_=b1[:, :], func=AF.Sigmoid), r=[k1], w=["pgs"])
                    P.op("dve", lambda e, b2=b2: e.tensor_tensor(out=pgs[:, :], in0=pgs[:, :], in1=b2[:, :], op=ALU.mult), r=["pgs", k2], w=["pgs"])
                    P.op("dve", lambda e, xs=xs: e.tensor_tensor(out=ho[xs][:, :], in0=pgs[:, :], in1=xin[xs][:, :], op=ALU.add), r=["pgs", "pxin%d" % xs], w=["pho%d" % xs])
                    P.op("act", lambda e, xs=xs: e.activation(out=sqj[:, :], in_=ho[xs][:, :], func=AF.Square), r=["pho%d" % xs], w=["psq"])
                    P.op("dve", lambda e, tt=tt, nb=nb: e.reduce_sum(out=ssp[:, tt, nb:nb + 1], in_=sqj[:, :], axis=AX.X), r=["psq"], w=["pssp"])
                    P.dma("sp", h3d[tt * 128:(tt + 1) * 128, nb * 512:(nb + 1) * 512], ho[xs][:, :], r=["pho%d" % xs], sem="d_h3_%d" % xs)
            P.op("dve", lambda e: e.reduce_sum(out=ss_all[:, 18:27], in_=ssp[:, :, :], axis=AX.X), r=["pssp"], w=["ss_all"])
            P.barrier()

    def final_phase():
        with ExitStack() as st:
            for sname in ["d_h3_0", "d_h3_1", "d_h3_2"]:
                wait_dram(sname)
            gb = sb(st, "fgb", [128, D])
            P.dma("sp", gb[:], gvec[3].partition_broadcast(128), w=["fgb"])
            xt = [sb(st, "fxt%d" % i, [128, D]) for i in range(2)]
            yo = [sb(st, "fyo%d" % i, [128, D]) for i in range(2)]
            rs = sb(st, "frs", [128, 9])
            P.op("act", lambda e: e.activation(out=rs[:, :], in_=ss_all[:, 18:27], func=AF.Sqrt, bias=EPS_RMS(), scale=1.0 / D), r=["ss_all", "pp"], w=["frs"])
            P.op("dve", lambda e: e.reciprocal(out=rs[:, :], in_=rs[:, :]), r=["frs"], w=["frs"])
            for tt in range(9):
                s = tt % 2
                P.dma("sp", xt[s][:, :], h3d[tt * 128:(tt + 1) * 128, :], w=["fxt%d" % s])
                P.op("dve", lambda e, s=s, tt=tt: e.scalar_tensor_tensor(out=yo[s][:, :], in0=xt[s][:, :], scalar=rs[:, tt:tt + 1], in1=gb[:, :], op0=ALU.mult, op1=ALU.mult), r=["fxt%d" % s, "frs", "fgb"], w=["fyo%d" % s])
                P.dma("sp", y_out[tt * 128:(tt + 1) * 128, :], yo[s][:, :], r=["fyo%d" % s], sem="d_yout%d" % s)
            P.barrier()

    nph = NPH
    if SUB == 10:
        nph = 0
    if nph >= 1:
        mixer_pass(False)
    if nph >= 2:
        mixer_pass(True)
    if nph >= 3:
        out_phase()
    if nph >= 4:
        ffn_up_phase()
    if nph >= 5:
        tm_gemm_phase("dn", actT, ["d_act0", "d_act1"], NFC, w_dn, h1d, ["d_h1_0", "d_h1_1", "d_h1_2"], h2d, "d_h2", 9, [(0, 5), (5, 4)], wcols=256, nslots=3)
    if nph >= 6:
        ple_phase()
    if nph >= 7:
        final_phase()
    P.barrier()
    es.close()
    return nc, P


_CACHE = {}


def _consts():
    ii = np.arange(128)
    part = ii[:, None]
    free = ii[None, :]
    U = (part < free).astype(np.float32)
    Ui = (part <= free).astype(np.float32)
    L = (part > free).astype(np.float32)
    same = ((part // 8) == (free // 8)).astype(np.float32)
    maskg = np.stack([np.concatenate([-U, Ui, U, Ui], axis=1),
                      np.concatenate([-U * same, Ui * same, U * same, Ui * same], axis=1)]).astype(np.float32)
    maskl = np.stack([np.concatenate([-L, -L], axis=1), np.concatenate([-L * same, -L * same], axis=1)]).astype(np.float32)
    blk = np.zeros((128, 128), np.float32)
    blk[:64, :64] = 1
    blk[64:, 64:] = 1
    segt = ((ii[:, None] // 8) == np.arange(NSEG)[None, :]).astype(np.float32)
    segf = np.broadcast_to(segt.T[None, :, :], (128, NSEG, 128)).reshape(128, NSEG * 128).astype(np.float32)
    rm = np.ones((128, TM), np.float32)
    rm[:, 0:TP:128] = 0
    rm[:, TP:TM:8] = 0
    return dict(ident=np.eye(128, dtype=np.float32), blk=blk, blks=blk / 64.0,
                onesc=np.full((128, 128), 1.0 / DR, np.float32), maskg=maskg, maskl=maskl,
                segf=np.ascontiguousarray(segf), segt=segt, rmask=rm)


def make_in_maps(inp, cores=None):
    f = lambda k: np.asarray(inp[k], dtype=np.float32)
    x_prompt, x_sample = f("x_prompt"), f("x_sample")
    state_wkv, state_shift, state_conv = f("state_wkv")[0], f("state_shift")[0], f("state_conv")[0]
    p_prompt, p_sample = f("p_prompt")[0], f("p_sample")[0]

    def col(v, n):
        return np.ascontiguousarray(np.asarray(v, np.float32).reshape(n, 128).T)
    pp = np.zeros((128, NPP), np.float32)

    def put(name, arr):
        o, w = PP[name]
        pp[:arr.shape[0], o:o + arr.shape[1]] = arr
    mu = f("mu_shift")[0]
    put("mu_r", col(mu[0:DR], 16))
    put("mu_k", col(mu[DR:2 * DR], 16))
    put("mu_v", col(mu[2 * DR:3 * DR], 16))
    put("mu_w", mu[3 * DR:3 * DR + 96].reshape(96, 1))
    put("mu_a", mu[3 * DR + 96:3 * DR + 192].reshape(96, 1))
    put("mu_g", col(mu[3 * DR + 192:3 * DR + 448], 2))
    for nm, key in [("w0", "w0"), ("a0", "a0"), ("k_k", "k_k"), ("k_a", "k_a"), ("lnx_w", "lnx_w"), ("lnx_b", "lnx_b"),
                    ("conv_b", "conv_b"), ("cln_w", "conv_ln_w"), ("cln_b", "conv_ln_b")]:
        put(nm, col(f(key)[0], 16))
    put("r_k", col(f("r_k")[0].reshape(-1), 16))
    cw = f("conv_w")[0]
    put("conv_w", np.ascontiguousarray(cw.T.reshape(16, 128, 31).transpose(1, 0, 2)).reshape(128, 16 * 31))
    put("eps", np.broadcast_to(np.array([1e-6, 64e-5, 1e-5, 0.0], np.float32), (128, 4)))
    gvec = np.stack([f("g_mix")[0], f("g_ffn")[0], f("g_ple")[0], f("g_final")])
    shared = dict(w_in=f("w_in")[0], w2=f("w2")[0], a2=f("a2")[0], g2=f("g2")[0], w_out=f("w_out")[0],
                  w_gu=f("w_gate_up")[0], w_dn=f("w_down")[0], w_pg=f("w_ple_gate")[0], w_pp=f("w_ple_proj")[0],
                  gvec=gvec, pp=pp, **_consts())
    in_maps = []
    for c in (range(NCORES) if cores is None else cores):
        b, hf = c // 2, c % 2
        sl = slice(16 * c, 16 * c + 16)
        m = dict(shared)
        m["x_main"] = np.concatenate([x_prompt[b, hf * TP:(hf + 1) * TP], x_sample[sl].reshape(TS, D)], axis=0)
        m["x_pre"] = np.ascontiguousarray(x_prompt[b, 0:TP]) if hf == 1 else np.zeros((TP, D), np.float32)
        m["p_main"] = np.concatenate([p_prompt[b, hf * TP:(hf + 1) * TP], p_sample[sl].reshape(TS, PLE)], axis=0)
        m["sshift"] = np.ascontiguousarray(state_shift[sl])
        m["sconv"] = np.ascontiguousarray(state_conv[sl].reshape(NSEG * 30, DR))
        sw = state_wkv[sl].reshape(NSEG, NHP, 2, 64, 64)
        m["swkvT"] = np.ascontiguousarray(sw.transpose(1, 2, 4, 0, 3).reshape(NHP, 128, NSEG, 64))
        in_maps.append(m)
    return in_maps


def kernel(**inp):
    in_maps = make_in_maps(inp)
    if "nc" not in _CACHE:
        _CACHE["nc"] = build()[0]
    res = run_bass_kernel_spmd(_CACHE["nc"], in_maps, core_ids=list(range(NCORES)))
    R = res.results
    y_prompt = np.zeros((4, 2048, D), np.float32)
    y_sample = np.zeros((128, 8, D), np.float32)
    wkv_p = np.zeros((1, 4, 32, 64, 64), np.float32)
    shift_p = np.zeros((1, 4, D), np.float32)
    conv_p = np.zeros((1, 4, 30, DR), np.float32)
    wkv_s = np.zeros((1, 128, 32, 64, 64), np.float32)
    shift_s = np.zeros((1, 128, D), np.float32)
    conv_s = np.zeros((1, 128, 30, DR), np.float32)
    for c in range(NCORES):
        b, hf = c // 2, c % 2
        sl = slice(16 * c, 16 * c + 16)
        r = R[c]
        y_prompt[b, hf * TP:(hf + 1) * TP] = r["y_out"][0:TP]
        y_sample[sl] = r["y_out"][TP:TM].reshape(16, 8, D)
        shift_s[0, sl] = r["shift_out"][0:NSEG]
        conv_s[0, sl] = r["convs_out"]
        ws = r["wkvs_out"].reshape(NHP, 2, 64, NSEG, 64)
        wkv_s[0, sl] = ws.transpose(3, 0, 1, 4, 2).reshape(NSEG, 32, 64, 64)
        if hf == 1:
            shift_p[0, b] = r["shift_out"][NSEG]
            conv_p[0, b] = r["convp_out"]
            wp = r["wkvp_out"].reshape(NHP, 2, 64, 64)
            wkv_p[0, b] = wp.transpose(0, 1, 3, 2).reshape(32, 64, 64)
    return (y_prompt, y_sample, wkv_p, shift_p, conv_p, wkv_s, shift_s, conv_s)
```

```python
import math
import numpy as np
from contextlib import ExitStack
import concourse.bass as bass
import concourse.mybir as mybir
from concourse.bass_utils import run_bass_kernel_spmd

F32 = mybir.dt.float32
BF16 = mybir.dt.bfloat16
AF = mybir.ActivationFunctionType
ALU = mybir.AluOpType
AX = mybir.AxisListType

D = 4096
DR = 2048
NHP = 16
DFF = 11008
NFC = 86
PLE = 256
RW = 6592
INC = 10688
TP = 1024
TS = 128
TM = TP + TS
NSEG = 16
EXM = 32 + TP + NSEG * 9
EXP = 32 + TP
STRICT = True
NPH = 7
SUB = 0
NCORES = 8

PP = {}
_o = 0
for _n, _w in [("mu_r", 16), ("mu_k", 16), ("mu_v", 16), ("mu_w", 1), ("mu_a", 1), ("mu_g", 2),
               ("w0", 16), ("a0", 16), ("k_k", 16), ("k_a", 16), ("r_k", 16), ("lnx_w", 16),
               ("lnx_b", 16), ("conv_b", 16), ("cln_w", 16), ("cln_b", 16), ("conv_w", 16 * 31),
               ("eps", 4)]:
    PP[_n] = (_o, _w)
    _o += _w
NPP = _o


class Prog:
    ENG = ("pe", "act", "dve", "pool", "sp")

    def __init__(self, nc, es):
        self.nc = nc
        self.es = es
        self.e = dict(pe=nc.tensor, act=nc.scalar, dve=nc.vector, pool=nc.gpsimd, sp=nc.sync)
        self.sem = {k: es.enter_context(nc.semaphore("s_" + k)) for k in self.ENG}
        self.cnt = {k: 0 for k in self.ENG}
        self.seen = {k: {} for k in self.ENG}
        self.dsem = {}
        self.dcnt = {}
        self.lastw = {}
        self.readers = {}
        self.nops = 0
        self.n2p = {}
        self.nq = {}
        self.swq = []

    def _wait(self, eng, tok):
        if tok is None:
            return
        kind, key, val = tok
        if kind == "e" and key == eng and (eng == "pe" or not STRICT):
            return
        if self.seen[eng].get((kind, key), 0) >= val:
            return
        s = self.sem[key] if kind == "e" else self.dsem[key]
        self.e[eng].wait_ge(s, val)
        self.seen[eng][(kind, key)] = val

    def _deps(self, eng, r, w):
        for k in list(r) + list(w):
            self._wait(eng, self.lastw.get(k))
        for k in w:
            for tok in self.readers.get(k, {}).values():
                self._wait(eng, tok)

    def _commit(self, tok, r, w):
        for k in w:
            self.lastw[k] = tok
            self.readers[k] = {}
        for k in r:
            self.readers.setdefault(k, {})[(tok[0], tok[1])] = tok

    def op(self, eng, fn, r=(), w=()):
        psr = [k for k in r if k.startswith("ps")]
        if psr:
            r = [k for k in r if not k.startswith("ps")]
            w = list(w) + psr
        self._deps(eng, r, w)
        inst = fn(self.e[eng])
        self.cnt[eng] += 1
        inst.then_inc(self.sem[eng], 1)
        self._commit(("e", eng, self.cnt[eng]), r, w)
        self.nops += 1

    def dma(self, q, out, in_, r=(), w=(), sem=None):
        semname = self.semname(sem or ("d_" + (w[0] if w else r[0])), q)
        self._deps(q, r, w)
        if self.dcnt[semname] > 0:
            self._wait(q, ("d", semname, self.dcnt[semname]))
        if q == "pool":
            nd = 1
            for d_ in list(out.shape)[:-1]:
                nd *= int(d_)
            nd = nd // 16 + 3
            while self.swq and sum(x[2] for x in self.swq) + nd > 1000:
                sn_, cv_, _ = self.swq.pop(0)
                self._wait(q, ("d", sn_, cv_))
            self.swq.append((semname, self.dcnt[semname] + 16, nd))
        self.e[q].dma_start(out=out, in_=in_).then_inc(self.dsem[semname], 16)
        self.dcnt[semname] += 16
        self._commit(("d", semname, self.dcnt[semname]), r, w)
        self.nops += 1

    def semname(self, name, q):
        if name not in self.n2p:
            self.nq[q] = self.nq.get(q, 0) + 1
            if q == "pool":
                pname = "dq%d" % (self.nq[q] % 30)
            else:
                pname = "dp%d" % (self.nq[q] % 60)
            if pname not in self.dsem:
                self.dsem[pname] = self.es.enter_context(self.nc.semaphore(pname))
                self.dcnt[pname] = 0
            self.n2p[name] = pname
        return self.n2p[name]

    def barrier(self):
        for eng in self.ENG:
            for e2 in self.ENG:
                if e2 != eng and self.cnt[e2] > 0:
                    self._wait(eng, ("e", e2, self.cnt[e2]))
            for sn, v in self.dcnt.items():
                if v > 0:
                    self._wait(eng, ("d", sn, v))
        self.lastw = {}
        self.readers = {}


def build():
    nc = bass.Bass("TRN2", target_bir_lowering=False)

    def din(name, shape, dt=F32):
        return nc.dram_tensor(name, list(shape), dt, kind="ExternalInput").ap()

    def dout(name, shape, dt=F32):
        return nc.dram_tensor(name, list(shape), dt, kind="ExternalOutput").ap()

    def dscr(name, shape, dt=F32):
        return nc.dram_tensor(name, list(shape), dt).ap()

    x_main = din("x_main", [TM, D])
    x_pre = din("x_pre", [TP, D])
    p_main = din("p_main", [TM, PLE])
    sshift = din("sshift", [NSEG, D])
    sconv = din("sconv", [NSEG * 30, DR])
    swkvT = din("swkvT", [NHP, 128, NSEG, 64])
    w_in = din("w_in", [D, INC])
    w2 = din("w2", [96, DR])
    a2 = din("a2", [96, DR])
    g2 = din("g2", [256, DR])
    w_out = din("w_out", [D, D])
    w_gu = din("w_gu", [D, 2 * DFF])
    w_dn = din("w_dn", [DFF, D])
    w_pg = din("w_pg", [D, D])
    w_pp = din("w_pp", [PLE, D])
    gvec = din("gvec", [4, D])
    pp_d = din("pp", [128, NPP])
    ident_d = din("ident", [128, 128])
    blk_d = din("blk", [128, 128])
    blks_d = din("blks", [128, 128])
    onesc_d = din("onesc", [128, 128])
    maskg_d = din("maskg", [2, 128, 4 * 128])
    maskl_d = din("maskl", [2, 128, 2 * 128])
    segf_d = din("segf", [128, NSEG * 128])
    segt_d = din("segt", [128, NSEG])
    rmask_d = din("rmask", [128, TM])

    y_out = dout("y_out", [TM, D])
    shift_out = dout("shift_out", [NSEG + 1, D])
    convp_out = dout("convp_out", [30, DR])
    convs_out = dout("convs_out", [NSEG, 30, DR])
    wkvp_out = dout("wkvp_out", [NHP, 128, 64])
    wkvs_out = dout("wkvs_out", [NHP, 128, NSEG, 64])

    catT = dscr("catT", [D, TM], BF16)
    cTd = dscr("cTd", [DR, TM], F32)
    h1d = dscr("h1d", [TM, D], F32)
    actT = dscr("actT", [DFF, TM], BF16)
    h2d = dscr("h2d", [TM, D], F32)
    h3d = dscr("h3d", [TM, D], F32)

    es = ExitStack()
    P = Prog(nc, es)

    def sb(st, name, shape, dt=F32):
        return st.enter_context(nc.sbuf_tensor("t_" + name, list(shape), dt))

    zA = es.enter_context(nc.psum_tensor("zA", [128, 1536], F32))
    zB = es.enter_context(nc.psum_tensor("zB", [128, 1536], F32))
    m0 = es.enter_context(nc.psum_tensor("m0", [128, 512], F32))
    m1 = es.enter_context(nc.psum_tensor("m1", [128, 512], F32))

    def bank(i):
        if i < 3:
            return zA[:, i * 512:(i + 1) * 512], "ps%d" % i
        if i < 6:
            return zB[:, (i - 3) * 512:(i - 2) * 512], "ps%d" % i
        return (m0 if i == 6 else m1)[:, :], "ps%d" % i

    ident_f = sb(es, "ident_f", [128, 128])
    ident_b = sb(es, "ident_b", [128, 128], BF16)
    pp = sb(es, "pp", [128, NPP])
    omka = sb(es, "omka", [128, 16])
    P.dma("sp", ident_f[:], ident_d, w=["ident_f"])
    P.dma("pool", ident_b[:], ident_d, w=["ident_b"])
    P.dma("sp", pp[:], pp_d, w=["pp"])

    def ppc(name, j=0, n=1, rows=128):
        o, w = PP[name]
        return pp[0:rows, o + j:o + j + n]

    P.op("dve", lambda e: e.tensor_scalar(out=omka[:], in0=pp[:, PP["k_a"][0]:PP["k_a"][0] + 16],
                                          scalar1=-1.0, scalar2=1.0, op0=ALU.mult, op1=ALU.add),
         r=["pp"], w=["omka"])
    EPS_RMS = lambda rows=128: ppc("eps", 0, 1, rows)
    EPS_GN = lambda rows=128: ppc("eps", 1, 1, rows)
    EPS_LN = lambda rows=128: ppc("eps", 2, 1, rows)


    def gemm_fm(wt, wkey, ncols, xT, xkey, E, bank0, nk=32, rows=128):
        nb = (E + 511) // 512
        for k in range(nk):
            for j in range(nb):
                c0 = j * 512
                c1 = min(E, c0 + 512)
                bap, bkey = bank(bank0 + j)
                P.op("pe", lambda e, k=k, c0=c0, c1=c1, bap=bap: e.matmul(out=bap[0:ncols, 0:c1 - c0], lhsT=wt[0:rows, k, 0:ncols], rhs=xT[0:rows, k, c0:c1], start=(k == 0), stop=(k == nk - 1)),
                     r=[wkey, xkey], w=[bkey])

    def zps(bank0, ncols, E):
        t = zA if bank0 == 0 else zB
        return t[0:ncols, 0:E], ["ps%d" % (bank0 + j) for j in range((E + 511) // 512)]

    wslots = []
    wstate = {"i": 0}

    def load_wcol(c0, ncols):
        i = wstate["i"] % len(wslots)
        wstate["i"] += 1
        t = wslots[i]
        key = "wslot%d" % i
        P.dma("pool", t[:, :, 0:ncols], w_in[:, c0:c0 + ncols].rearrange("(k p) n -> p k n", p=128), w=[key])
        return t, key

    S_f = sb(es, "S_f", [128, NHP, 64])
    histT = sb(es, "histT", [128, 32, 32], BF16)
    P.op("dve", lambda e: e.memset(S_f[:], 0.0), w=["S_f"])

    def mixer_pass(is_main):
        E = EXM if is_main else EXP
        TY = TM if is_main else TP
        nchunks = 9 if is_main else 8
        tag = "m" if is_main else "p"
        with ExitStack() as st:
            xT = sb(st, tag + "xT", [128, 32, E], BF16)
            xkey = tag + "xT"
            with ExitStack() as sa:
                if is_main:
                    P.op("dve", lambda e: e.tensor_copy(out=xT[:, :, 0:32], in_=histT[:]), r=["histT"], w=[xkey])

                    def dst_fn(tt, kk):
                        if tt < 8:
                            return xT[:, kk * 8:(kk + 1) * 8, 32 + tt * 128:32 + (tt + 1) * 128]
                        v = xT[:, kk * 8:(kk + 1) * 8, 32 + TP:E].rearrange("p q (s t) -> p q s t", t=9)[:, :, :, 1:9]
                        return v

                    def fp32_rows(tt, xn32, query):
                        if query:
                            return tt >= 7
                        if tt == 7:
                            P.dma("sp", shift_out[NSEG:NSEG + 1, :], xn32[127:128, :], r=["mA_xn32"], sem="d_shout")
                        else:
                            for sq_ in range(NSEG):
                                P.dma("sp", shift_out[sq_:sq_ + 1, :], xn32[sq_ * 8 + 7:sq_ * 8 + 8, :], r=["mA_xn32"], sem="d_shout%d" % (sq_ % 4))
                        return True

                    def src_rows(tt):
                        return x_main[tt * 128:(tt + 1) * 128, :]
                    norm_transpose_main(sa, src_rows, 9, 0, xT, xkey, dst_fn, "mA", fp32_rows)
                    norm_transpose_shift(sa, xT, xkey)
                else:
                    P.op("dve", lambda e: e.memset(xT[:, :, 0:32], 0.0), w=[xkey])

                    def dst_fn(tt, kk):
                        return xT[:, kk * 8:(kk + 1) * 8, 32 + tt * 128:32 + (tt + 1) * 128]
                    norm_transpose_main(sa, lambda tt: x_pre[tt * 128:(tt + 1) * 128, :], 8, 0, xT, xkey, dst_fn, "pA", None)
                    P.op("dve", lambda e: e.tensor_copy(out=histT[:], in_=xT[:, :, E - 32:E]), r=[xkey], w=["histT"])
                P.barrier()
            if SUB == 1 or SUB >= 11:
                return
            with ExitStack() as sbk:
                rwkv_phase(sbk, is_main, xT, xkey, E, TY, nchunks, tag)
                P.barrier()
            if is_main:
                with ExitStack() as sc:
                    conv_phase(sc, xT, xkey, E)
                    P.barrier()

    def norm_transpose_main(st, src_rows, ntiles, gcol, xT, xkey, dst_fn, tag, fp32_rows):
        def dst2(tt, kk):
            return dst_fn(tt, kk)
        norm_transpose_impl(st, src_rows, ntiles, gcol, xT, xkey, dst2, tag, fp32_rows, sample_tile=(8 if ntiles == 9 else None))

    def norm_transpose_impl(st, src_rows_fn, ntiles, gcol, xT, xT_key, dst_fn, tag, fp32_rows, sample_tile):
        gb = sb(st, tag + "_gb", [128, D])
        P.dma("sp", gb[:], gvec[gcol].partition_broadcast(128), w=[tag + "_gb"])
        xt = [sb(st, tag + "_xt%d" % i, [128, D]) for i in range(2)]
        sqf = sb(st, tag + "_sqf", [128, D])
        xnb = [sb(st, tag + "_xnb%d" % i, [128, D], BF16) for i in range(2)]
        xn32 = sb(st, tag + "_xn32", [128, D]) if fp32_rows is not None else None
        stt = sb(st, tag + "_stt", [128, 4 * ntiles])
        for tt in range(ntiles if SUB < 11 else SUB - 10):
            s = tt % 2
            xk = tag + "_xt%d" % s
            P.dma("sp", xt[s][:, :], src_rows_fn(tt), w=[xk])
            ssc = stt[:, 4 * tt:4 * tt + 1]
            stdc = stt[:, 4 * tt + 1:4 * tt + 2]
            rsc = stt[:, 4 * tt + 2:4 * tt + 3]
            sk = tag + "_stt%d" % tt
            P.op("act", lambda e, s=s: e.activation(out=sqf[:, :], in_=xt[s][:, :], func=AF.Square), r=[xk], w=[tag + "_sqf"])
            P.op("dve", lambda e, ssc=ssc: e.reduce_sum(out=ssc, in_=sqf[:, :], axis=AX.X), r=[tag + "_sqf"], w=[sk])
            P.op("act", lambda e, ssc=ssc, stdc=stdc: e.activation(out=stdc, in_=ssc, func=AF.Sqrt, bias=EPS_RMS(), scale=1.0 / D), r=[sk, "pp"], w=[sk])
            P.op("dve", lambda e, rsc=rsc, stdc=stdc: e.reciprocal(out=rsc, in_=stdc), r=[sk], w=[sk])
            P.op("dve", lambda e, s=s, rsc=rsc: e.scalar_tensor_tensor(out=xnb[s][:, :], in0=xt[s][:, :], scalar=rsc, in1=gb[:, :], op0=ALU.mult, op1=ALU.mult),
                 r=[xk, sk, tag + "_gb"], w=[tag + "_xnb%d" % s])
            if fp32_rows is not None and fp32_rows(tt, None, True):
                P.op("dve", lambda e, s=s, rsc=rsc: e.scalar_tensor_tensor(out=xn32[:, :], in0=xt[s][:, :], scalar=rsc, in1=gb[:, :], op0=ALU.mult, op1=ALU.mult),
                     r=[xk, sk, tag + "_gb"], w=[tag + "_xn32"])
                fp32_rows(tt, xn32, False)
            transpose_tile(xnb[s], tag + "_xnb%d" % s, 128, lambda kk, tt=tt: dst_fn(tt, kk), xT_key, is_sample=(tt == sample_tile))

    def transpose_tile(src, skey, rows, dst_of_kk, xT_key, is_sample=False, nk=32):
        for kk in range(nk // 8):
            bi = 6 + (kk % 2)
            bap, bkey = bank(bi)
            pT = bap.bitcast(BF16)
            for q in range(8):
                kc = kk * 8 + q
                P.op("pe", lambda e, kc=kc, q=q, pT=pT: e.transpose(out=pT[:, q * 128:q * 128 + rows], in_=src[0:rows, kc * 128:(kc + 1) * 128], identity=ident_b[0:rows, 0:rows]),
                     r=[skey, "ident_b"], w=[bkey])
            srcv = pT.rearrange("p (q t) -> p q t", q=8)[:, :, 0:rows]
            if is_sample:
                srcv = srcv.rearrange("p q (s t) -> p q s t", t=8)
            dst = dst_of_kk(kk)
            if kk % 2 == 0:
                P.op("act", lambda e, srcv=srcv, dst=dst: e.activation(out=dst, in_=srcv, func=AF.Copy), r=[bkey], w=[xT_key])
            else:
                P.op("dve", lambda e, srcv=srcv, dst=dst: e.tensor_copy(out=dst, in_=srcv), r=[bkey], w=[xT_key])

    def norm_transpose_shift(st, xT, xkey):
        t32 = sb(st, "sh32", [NSEG, D])
        tb = sb(st, "shb", [NSEG, D], BF16)
        P.dma("sp", t32[:], sshift, w=["sh32"])
        P.op("dve", lambda e: e.tensor_copy(out=tb[:], in_=t32[:]), r=["sh32"], w=["shb"])

        def dst(kk):
            return xT[:, kk * 8:(kk + 1) * 8, 32 + TP:EXM].rearrange("p q (s t) -> p q s t", t=9)[:, :, :, 0]
        transpose_tile(tb, "shb", NSEG, dst, xkey)

    def rwkv_phase(st, is_main, xT, xkey, E, TY, nchunks, tag):
        nonlocal wslots
        wslots = [sb(st, tag + "ws%d" % i, [128, 32, 128], BF16) for i in range(2)]
        wstate["i"] = 0
        NPC = 8
        w2bs = [sb(st, tag + "w2b%d" % i, [96, 128], BF16) for i in range(2)]
        a2bs = [sb(st, tag + "a2b%d" % i, [96, 128], BF16) for i in range(2)]
        g2bs = [sb(st, tag + "g2b%d" % i, [128, 2, 128], BF16) for i in range(2)]
        blk = sb(st, tag + "blk", [128, 128])
        blks = sb(st, tag + "blks", [128, 128])
        P.dma("sp", blk[:], blk_d, w=["blk"])
        P.dma("sp", blks[:], blks_d, w=["blks"])
        rmask = sb(st, tag + "rmask", [128, TY], BF16)
        P.dma("pool", rmask[:], rmask_d[:, 0:TY], w=["rmask"])
        maskg_p = sb(st, tag + "mgp", [128, 512], BF16)
        maskl_p = sb(st, tag + "mlp", [128, 256], BF16)
        P.dma("pool", maskg_p[:], maskg_d[0], w=["mgp"])
        P.dma("pool", maskl_p[:], maskl_d[0], w=["mlp"])
        if is_main:
            maskg_s = sb(st, tag + "mgs", [128, 512], BF16)
            maskl_s = sb(st, tag + "mls", [128, 256], BF16)
            segf = sb(st, tag + "segf", [128, NSEG, 128], BF16)
            segt = sb(st, tag + "segt", [128, NSEG], BF16)
            P.dma("pool", maskg_s[:], maskg_d[1], w=["mgs"])
            P.dma("pool", maskl_s[:], maskl_d[1], w=["mls"])
            P.dma("pool", segf[:], segf_d.rearrange("p (s t) -> p s t", s=NSEG), w=["segf"])
            P.dma("pool", segt[:], segt_d, w=["segt"])

        t2f = sb(st, tag + "t2f", [128, E])
        t3f = sb(st, tag + "t3f", [128, E])
        zs, dX = t2f, t3f

        def mix(zbank0, ncols, mu_ap, out_t, okey, func=None):
            zp, zkeys = zps(zbank0, ncols, E)
            P.op("act", lambda e: e.activation(out=zs[0:ncols, :], in_=zp, func=AF.Copy), r=zkeys, w=["t2"])
            P.op("dve", lambda e: e.tensor_tensor(out=dX[0:ncols, 0:E - 32], in0=zs[0:ncols, 31:E - 1], in1=zs[0:ncols, 32:E], op=ALU.subtract), r=["t2"], w=["t3"])
            dst = out_t if func is None else dX
            tgt = out_t if func is None else zs
            if func is None:
                P.op("dve", lambda e: e.scalar_tensor_tensor(out=out_t[0:ncols, 0:TP], in0=dX[0:ncols, 0:TP], scalar=mu_ap, in1=zs[0:ncols, 32:32 + TP], op0=ALU.mult, op1=ALU.add),
                     r=["t3", "t2", "pp"], w=[okey])
                if is_main:
                    P.op("dve", lambda e: e.scalar_tensor_tensor(
                        out=out_t[0:ncols, TP:TM].rearrange("p (s t) -> p s t", t=8),
                        in0=dX[0:ncols, TP:TP + 144].rearrange("p (s t) -> p s t", t=9)[:, :, 1:9], scalar=mu_ap,
                        in1=zs[0:ncols, 32 + TP:E].rearrange("p (s t) -> p s t", t=9)[:, :, 1:9], op0=ALU.mult, op1=ALU.add),
                        r=["t3", "t2", "pp"], w=[okey])
            else:
                P.op("dve", lambda e: e.scalar_tensor_tensor(out=dX[0:ncols, 0:E - 32], in0=dX[0:ncols, 0:E - 32], scalar=mu_ap, in1=zs[0:ncols, 32:E], op0=ALU.mult, op1=ALU.add),
                     r=["t3", "t2", "pp"], w=["t3"])
                P.op("act", lambda e: e.activation(out=out_t[0:ncols, 0:TP], in_=dX[0:ncols, 0:TP], func=func), r=["t3"], w=[okey])
                if is_main:
                    P.op("act", lambda e: e.activation(out=out_t[0:ncols, TP:TM].rearrange("p (s t) -> p s t", t=8),
                                                       in_=dX[0:ncols, TP:TP + 144].rearrange("p (s t) -> p s t", t=9)[:, :, 1:9], func=func), r=["t3"], w=[okey])

        twT = sb(st, tag + "twT", [96, TY], BF16)
        zaT = sb(st, tag + "zaT", [96, TY], BF16)
        sgT = sb(st, tag + "sgT", [128, 2, TY], BF16)
        lo = 3 * DR
        for (c0, ncols, mu_ap, out_v, okey, func, b0) in [
            (lo, 96, ppc("mu_w", 0, 1, 96), twT, "twT", AF.Tanh, 0),
            (lo + 96, 96, ppc("mu_a", 0, 1, 96), zaT, "zaT", AF.Copy, 3),
            (lo + 192, 128, ppc("mu_g", 0, 1), None, "sgT", AF.Sigmoid, 0),
            (lo + 320, 128, ppc("mu_g", 1, 1), None, "sgT", AF.Sigmoid, 3),
        ]:
            wt, wkey = load_wcol(c0, ncols)
            gemm_fm(wt, wkey, ncols, xT, xkey, E, b0)
            if out_v is None:
                kidx = 0 if c0 == lo + 192 else 1
                mix(b0, ncols, mu_ap, sgT[:, kidx, :], okey, func)
            else:
                mix(b0, ncols, mu_ap, out_v, okey, func)

        if SUB == 2:
            return
        def fm(name, dt=F32):
            return sb(st, tag + name, [128, TY], dt)
        r_t, k_t, v_t = fm("r_t"), fm("k_t"), fm("v_t")
        a_t, ld_t, cl_t = fm("a_t"), fm("ld_t"), fm("cl_t")
        t1 = fm("t1")
        t2, t3 = t2f[:, 0:TY], t3f[:, 0:TY]
        RT, KT, BT, KK = fm("RT", BF16), fm("KT", BF16), fm("BT", BF16), fm("KK", BF16)
        yT = a_t
        ob = KK
        tmpc = sb(st, tag + "tmpc", [128, 3, 128], BF16)
        NS = 5 if is_main else 8
        Gms = [[sb(st, tag + "Gm%d_%d" % (s_, h), [128, 4, 128], BF16) for h in range(2)] for s_ in range(NS)]
        lvs = [sb(st, tag + "lv%d" % s_, [128, 2, 3, 128], BF16) for s_ in range(NS)]
        tok3s = [sb(st, tag + "tok3_%d" % s_, [128, 3, 128], BF16) for s_ in range(NS)]
        RHSb = sb(st, tag + "RHSb", [128, 128], BF16)
        Ub = sb(st, tag + "Ub", [128, 128], BF16)
        S_b = sb(st, tag + "S_b", [128, 64], BF16)
        if is_main:
            KTseg = r_t[:, 0:1024].bitcast(BF16).rearrange("p (s t) -> p s t", s=NSEG)
            RTseg = KTseg
            BGseg = cl_t[:, 0:1024].bitcast(BF16).rearrange("p (s t) -> p s t", s=NSEG)
            KGseg = sb(st, tag + "KGseg", [128, NSEG, 128], BF16)
            Ss_f = sb(st, tag + "Ss_f", [128, NSEG, 64])
            Ss_n = Ss_f
            Ss_b = sb(st, tag + "Ss_b", [128, NSEG, 64], BF16)

        def ev(eng, fn, r, w):
            P.op(eng, fn, r=r, w=w)

        for hp in range(NHP):
            f0 = hp * 128
            lsl = hp % 2
            w2b, a2b, g2b = w2bs[lsl], a2bs[lsl], g2bs[lsl]
            w2k, a2k, g2k = "w2b%d" % lsl, "a2b%d" % lsl, "g2b%d" % lsl
            P.dma("pool", w2b[:, :], w2[:, f0:f0 + 128], w=[w2k])
            P.dma("pool", a2b[:, :], a2[:, f0:f0 + 128], w=[a2k])
            if is_main:
                P.dma("pool", g2b[:, :, :], g2[:, f0:f0 + 128].rearrange("(k p) n -> p k n", p=128), w=[g2k])
            for (nm, cbase, out_t, mu_nm, b0) in [("r", 0, r_t, "mu_r", 0), ("k", DR, k_t, "mu_k", 3), ("v", 2 * DR, v_t, "mu_v", 0)]:
                wt, wkey = load_wcol(cbase + f0, 128)
                gemm_fm(wt, wkey, 128, xT, xkey, E, b0)
                mix(b0, 128, ppc(mu_nm, hp, 1), out_t, nm + "_t")
            nbt = (TY + 511) // 512
            for j in range(nbt):
                c0, c1 = j * 512, min(TY, j * 512 + 512)
                bap, bkey = bank(3 + j)
                P.op("pe", lambda e, c0=c0, c1=c1, bap=bap: e.matmul(out=bap[:, 0:c1 - c0], lhsT=w2b[0:96, :], rhs=twT[0:96, c0:c1], start=True, stop=True), r=[w2k, "twT"], w=[bkey])
            zp, zkeys = zB[:, 0:TY], ["ps%d" % (3 + j) for j in range(nbt)]
            P.op("act", lambda e: e.activation(out=ld_t[:, :], in_=zp, func=AF.Sigmoid, bias=ppc("w0", hp, 1), scale=1.0), r=zkeys + ["pp"], w=["ld_t"])
            P.op("dve", lambda e: e.tensor_scalar(out=ld_t[:, :], in0=ld_t[:, :], scalar1=-math.exp(-0.5), scalar2=None, op0=ALU.mult), r=["ld_t"], w=["ld_t"])
            for j in range(nbt):
                c0, c1 = j * 512, min(TY, j * 512 + 512)
                bap, bkey = bank(0 + j)
                P.op("pe", lambda e, c0=c0, c1=c1, bap=bap: e.matmul(out=bap[:, 0:c1 - c0], lhsT=a2b[0:96, :], rhs=zaT[0:96, c0:c1], start=True, stop=True), r=[a2k, "zaT"], w=[bkey])
            zpa, zkeysa = zA[:, 0:TY], ["ps%d" % j for j in range(nbt)]
            P.op("act", lambda e: e.activation(out=a_t[:, :], in_=zpa, func=AF.Sigmoid, bias=ppc("a0", hp, 1), scale=1.0), r=zkeysa + ["pp"], w=["a_t"])
            P.op("dve", lambda e: e.tensor_scalar(out=t1[:, :], in0=k_t[:, :], scalar1=ppc("k_k", hp, 1), scalar2=None, op0=ALU.mult), r=["k_t", "pp"], w=["t1"])
            P.op("act", lambda e: e.activation(out=t2[:, :], in_=t1[:, :], func=AF.Square), r=["t1"], w=["t2"])
            for j in range(nbt):
                c0, c1 = j * 512, min(TY, j * 512 + 512)
                bap, bkey = bank(0 + j)
                P.op("pe", lambda e, c0=c0, c1=c1, bap=bap: e.matmul(out=bap[:, 0:c1 - c0], lhsT=blk[:, :], rhs=t2[:, c0:c1], start=True, stop=True), r=["blk", "t2"], w=[bkey])
            P.op("dve", lambda e: e.tensor_scalar(out=t2[:, :], in0=zpa, scalar1=1e-24, scalar2=None, op0=ALU.max), r=zkeysa, w=["t2"])
            P.op("act", lambda e: e.activation(out=t2[:, :], in_=t2[:, :], func=AF.Sqrt), r=["t2"], w=["t2"])
            P.op("dve", lambda e: e.reciprocal(out=t2[:, :], in_=t2[:, :]), r=["t2"], w=["t2"])
            P.op("dve", lambda e: e.tensor_tensor(out=t1[:, :], in0=t1[:, :], in1=t2[:, :], op=ALU.mult), r=["t1", "t2"], w=["t1"])
            P.op("dve", lambda e: e.tensor_tensor(out=t2[:, :], in0=t1[:, :], in1=a_t[:, :], op=ALU.mult), r=["t1", "a_t"], w=["t2"])
            P.op("dve", lambda e: e.tensor_scalar(out=t3[:, :], in0=a_t[:, :], scalar1=ppc("k_a", hp, 1), scalar2=omka[:, hp:hp + 1], op0=ALU.mult, op1=ALU.add), r=["a_t", "pp", "omka"], w=["t3"])
            P.op("dve", lambda e: e.tensor_tensor(out=k_t[:, :], in0=k_t[:, :], in1=t3[:, :], op=ALU.mult), r=["k_t", "t3"], w=["k_t"])
            P.op("dve", lambda e: e.tensor_tensor_scan(out=cl_t[:, :], data0=rmask[:, :], data1=ld_t[:, :], initial=0.0, op0=ALU.mult, op1=ALU.add), r=["rmask", "ld_t"], w=["cl_t"])
            P.op("act", lambda e: e.activation(out=t3[:, :], in_=cl_t[:, :], func=AF.Exp), r=["cl_t"], w=["t3"])
            P.op("dve", lambda e: e.tensor_tensor(out=RT[:, :], in0=r_t[:, :], in1=t3[:, :], op=ALU.mult), r=["r_t", "t3"], w=["RT"])
            P.op("dve", lambda e: e.tensor_tensor(out=ld_t[:, :], in0=cl_t[:, :], in1=ld_t[:, :], op=ALU.subtract), r=["cl_t", "ld_t"], w=["ld_t"])
            P.op("act", lambda e: e.activation(out=ld_t[:, :], in_=ld_t[:, :], func=AF.Exp), r=["ld_t"], w=["ld_t"])
            P.op("dve", lambda e: e.tensor_tensor(out=KT[:, :], in0=t1[:, :], in1=ld_t[:, :], op=ALU.mult), r=["t1", "ld_t"], w=["KT"])
            P.op("act", lambda e: e.activation(out=ld_t[:, :], in_=cl_t[:, :], func=AF.Exp, scale=-1.0), r=["cl_t"], w=["ld_t"])
            P.op("dve", lambda e: e.tensor_tensor(out=BT[:, :], in0=t2[:, :], in1=ld_t[:, :], op=ALU.mult), r=["t2", "ld_t"], w=["BT"])
            P.op("dve", lambda e: e.tensor_tensor(out=KK[:, :], in0=k_t[:, :], in1=ld_t[:, :], op=ALU.mult), r=["k_t", "ld_t"], w=["KK"])
            for c in range(NPC):
                P.op("act", lambda e, c=c: e.activation(out=ld_t[:, c * 128:(c + 1) * 128], in_=cl_t[:, c * 128:(c + 1) * 128], func=AF.Exp, bias=cl_t[:, c * 128 + 127:c * 128 + 128], scale=-1.0), r=["cl_t"], w=["ld_t"])
            if is_main:
                clv = cl_t[:, TP:TM].rearrange("p (s t) -> p s t", t=8)
                P.op("dve", lambda e: e.tensor_tensor(out=ld_t[:, TP:TM].rearrange("p (s t) -> p s t", t=8), in0=clv[:, :, 7:8].broadcast_to([128, NSEG, 8]), in1=clv, op=ALU.subtract), r=["cl_t"], w=["ld_t"])
                P.op("act", lambda e: e.activation(out=ld_t[:, TP:TM], in_=ld_t[:, TP:TM], func=AF.Exp), r=["ld_t"], w=["ld_t"])
            if is_main:
                P.op("dve", lambda e: e.scalar_tensor_tensor(out=t1[:, :], in0=r_t[:, :], scalar=ppc("r_k", hp, 1), in1=k_t[:, :], op0=ALU.mult, op1=ALU.mult), r=["r_t", "k_t", "pp"], w=["t1"])
            if is_main:
                P.dma("sp", Ss_f[:], swkvT[hp], w=["Ss_f"])
                P.op("act", lambda e: e.activation(out=Ss_b[:], in_=Ss_f[:], func=AF.Copy), r=["Ss_f"], w=["Ss_b"])
            P.op("act", lambda e: e.activation(out=S_b[:, :], in_=S_f[:, hp, :], func=AF.Copy), r=["S_f"], w=["S_b"])

            if SUB == 3:
                return
            def chunk_p12(c, slot):
                samp = (c == 8)
                cs = slice(c * 128, (c + 1) * 128)
                mg = maskg_s if samp else maskg_p
                ml = maskl_s if samp else maskl_p
                mgk, mlk = ("mgs", "mls") if samp else ("mgp", "mlp")
                tok3, tk = tok3s[slot], "tok3_%d" % slot
                bap6, bkey6 = bank(6)
                pT = bap6.bitcast(BF16)
                P.op("act", lambda e: e.activation(out=tmpc[:, 0, :], in_=v_t[:, cs], func=AF.Copy), r=["v_t"], w=["tmpc"])
                P.op("dve", lambda e: e.tensor_tensor(out=tmpc[:, 1, :], in0=t2[:, cs], in1=ld_t[:, cs], op=ALU.mult), r=["t2", "ld_t"], w=["tmpc"])
                P.op("dve", lambda e: e.tensor_tensor(out=tmpc[:, 2, :], in0=k_t[:, cs], in1=ld_t[:, cs], op=ALU.mult), r=["k_t", "ld_t"], w=["tmpc"])
                for i3 in range(3):
                    P.op("pe", lambda e, i3=i3: e.transpose(out=pT[:, i3 * 128:(i3 + 1) * 128], in_=tmpc[:, i3, :], identity=ident_b[:, :]), r=["tmpc", "ident_b"], w=[bkey6])
                P.op("act", lambda e: e.activation(out=tok3[:, :, :], in_=pT[:, 0:384].rearrange("p (a t) -> p a t", a=3), func=AF.Copy), r=[bkey6], w=[tk])
                for h in range(2):
                    hs = slice(h * 64, (h + 1) * 64)
                    bap, bkey = bank(0 + h)
                    G, gk = Gms[slot][h], "Gm%d_%d" % (slot, h)
                    L, lk_ = lvs[slot][:, h, :, :], "lv%d" % slot
                    for i4, (lh, lk, rh, rk) in enumerate([(BT, "BT", KT, "KT"), (BT, "BT", RT, "RT"), (KK, "KK", KT, "KT"), (KK, "KK", RT, "RT")]):
                        P.op("pe", lambda e, i4=i4, lh=lh, rh=rh, bap=bap, hs=hs: e.matmul(out=bap[:, i4 * 128:(i4 + 1) * 128], lhsT=lh[hs, cs], rhs=rh[hs, cs], start=True, stop=True), r=[lk, rk], w=[bkey])
                    P.op("dve", lambda e, G=G, bap=bap: e.tensor_tensor(out=G[:, :, :], in0=bap[:, :].rearrange("p (a t) -> p a t", a=4), in1=mg[:, :].rearrange("p (a t) -> p a t", a=4), op=ALU.mult), r=[bkey, mgk], w=[gk])
                    bap2, bkey2 = bank(2 if h == 0 else 7)
                    P.op("pe", lambda e, hs=hs, bap2=bap2: e.matmul(out=bap2[:, 0:128], lhsT=KT[hs, cs], rhs=BT[hs, cs], start=True, stop=True), r=["KT", "BT"], w=[bkey2])
                    P.op("dve", lambda e, L=L, bap2=bap2: e.tensor_tensor(out=L[:, 0, :], in0=bap2[:, 0:128], in1=ml[:, 0:128], op=ALU.mult), r=[bkey2, mlk], w=[lk_])
                    P.op("act", lambda e, L=L, G=G: e.activation(out=L[:, 1, :], in_=G[:, 0, :], func=AF.Copy), r=[gk], w=[lk_])
                    P.op("dve", lambda e, L=L, G=G: e.tensor_tensor(out=L[:, 2, :], in0=G[:, 0, :], in1=ident_b[:, :], op=ALU.add), r=[gk, "ident_b"], w=[lk_])

            def chunk_level(slot):
                L, lk_ = lvs[slot], "lv%d" % slot
                bq, bqk = bank([3, 4][slot % 2])
                bt, btk = bank([0, 1][slot % 2])
                for h in range(2):
                    Pm, Qm = L[:, h, 0, :], L[:, h, 1, :]
                    P.op("pe", lambda e, h=h, Pm=Pm, Qm=Qm: e.matmul(out=bq[:, (2 * h) * 128:(2 * h + 1) * 128], lhsT=Qm, rhs=Pm, start=True, stop=True), r=[lk_], w=[bqk])
                    P.op("pe", lambda e, h=h, Pm=Pm, Qm=Qm: e.matmul(out=bq[:, (2 * h + 1) * 128:(2 * h + 2) * 128], lhsT=Pm, rhs=Qm, start=True, stop=True), r=[lk_], w=[bqk])
                P.op("act", lambda e: e.activation(out=L[:, :, 0:2, :], in_=bq[:, :].rearrange("p (h a t) -> p h a t", h=2, a=2), func=AF.Copy), r=[bqk], w=[lk_])
                for h in range(2):
                    P.op("pe", lambda e, h=h: e.matmul(out=bt[:, h * 128:(h + 1) * 128], lhsT=L[:, h, 0, :], rhs=L[:, h, 2, :], start=True, stop=True), r=[lk_], w=[btk])
                P.op("dve", lambda e: e.tensor_tensor(out=L[:, :, 2, :], in0=L[:, :, 2, :], in1=bt[:, 0:256].rearrange("p (h t) -> p h t", h=2), op=ALU.add), r=[btk, lk_], w=[lk_])

            def chunk_chain(c, slot):
                samp = (c == 8)
                cs = slice(c * 128, (c + 1) * 128)
                tok3, tk = tok3s[slot], "tok3_%d" % slot
                Gm = Gms[slot]
                gks = ["Gm%d_%d" % (slot, h) for h in range(2)]
                TT = [lvs[slot][:, h, 2, :] for h in range(2)]
                TTk = ["lv%d" % slot for h in range(2)]
                if samp:
                    P.op("dve", lambda e: e.tensor_tensor(out=BGseg[:, :, :], in0=tok3[:, 1:2, :].broadcast_to([128, NSEG, 128]), in1=segt[:, :].unsqueeze(2).broadcast_to([128, NSEG, 128]), op=ALU.mult), r=[tk, "segt"], w=["cl_t"])
                    P.op("dve", lambda e: e.tensor_tensor(out=KGseg[:, :, :], in0=tok3[:, 2:3, :].broadcast_to([128, NSEG, 128]), in1=segt[:, :].unsqueeze(2).broadcast_to([128, NSEG, 128]), op=ALU.mult), r=[tk, "segt"], w=["KGseg"])
                    P.op("dve", lambda e: e.tensor_tensor(out=KTseg[:, :, :], in0=KT[:, cs].unsqueeze(1).broadcast_to([128, NSEG, 128]), in1=segf[:, :, :], op=ALU.mult), r=["KT", "segf"], w=["r_t"])
                bap5, bkey5 = bank(5)
                for h in range(2):
                    hs = slice(h * 64, (h + 1) * 64)
                    o = bap5[:, h * 64:(h + 1) * 64]
                    if samp:
                        for sg_ in range(NSEG):
                            P.op("pe", lambda e, sg_=sg_, hs=hs, o=o: e.matmul(out=o, lhsT=KTseg[hs, sg_, :], rhs=Ss_b[hs, sg_, :], start=(sg_ == 0), stop=False), r=["r_t", "Ss_b"], w=[bkey5])
                    else:
                        P.op("pe", lambda e, hs=hs, o=o: e.matmul(out=o, lhsT=KT[hs, cs], rhs=S_b[hs, :], start=True, stop=False), r=["KT", "S_b"], w=[bkey5])
                    P.op("pe", lambda e, h=h, hs=hs, o=o: e.matmul(out=o, lhsT=Gm[h][:, 2, :], rhs=tok3[:, 0, hs], start=False, stop=True), r=[gks[h], tk], w=[bkey5])
                P.op("act", lambda e: e.activation(out=RHSb[:, :], in_=bap5[:, 0:128], func=AF.Copy, scale=-1.0), r=[bkey5], w=["RHSb"])
                for h in range(2):
                    hs = slice(h * 64, (h + 1) * 64)
                    P.op("pe", lambda e, h=h, hs=hs: e.matmul(out=bap5[:, 128 + h * 64:128 + (h + 1) * 64], lhsT=TT[h], rhs=RHSb[:, hs], start=True, stop=True), r=[TTk[h], "RHSb"], w=[bkey5])
                P.op("dve", lambda e: e.tensor_copy(out=Ub[:, :], in_=bap5[:, 128:256]), r=[bkey5], w=["Ub"])
                if is_main:
                    if samp:
                        P.op("dve", lambda e: e.tensor_tensor(out=RTseg[:, :, :], in0=RT[:, cs].unsqueeze(1).broadcast_to([128, NSEG, 128]), in1=segf[:, :, :], op=ALU.mult), r=["RT", "segf"], w=["r_t"])
                    yb, ykey = bank(2)
                    yo = yb[:, 256:384]
                    for h in range(2):
                        hs = slice(h * 64, (h + 1) * 64)
                        o = yo[hs, :]
                        P.op("pe", lambda e, h=h, hs=hs, o=o: e.matmul(out=o, lhsT=Ub[:, hs], rhs=Gm[h][:, 1, :], start=True, stop=False), r=["Ub", gks[h]], w=[ykey])
                        P.op("pe", lambda e, h=h, hs=hs, o=o: e.matmul(out=o, lhsT=tok3[:, 0, hs], rhs=Gm[h][:, 3, :], start=False, stop=False), r=[tk, gks[h]], w=[ykey])
                        if samp:
                            for sg_ in range(NSEG):
                                P.op("pe", lambda e, sg_=sg_, hs=hs, o=o: e.matmul(out=o, lhsT=Ss_b[hs, sg_, :], rhs=RTseg[hs, sg_, :], start=False, stop=(sg_ == NSEG - 1)), r=["Ss_b", "r_t"], w=[ykey])
                        else:
                            P.op("pe", lambda e, hs=hs, o=o: e.matmul(out=o, lhsT=S_b[hs, :], rhs=RT[hs, cs], start=False, stop=True), r=["S_b", "RT"], w=[ykey])
                    P.op("act", lambda e: e.activation(out=yT[:, cs], in_=yo, func=AF.Copy), r=[ykey], w=["a_t"])
                if samp:
                    sbank = [bank(6), bank(7)]
                    for h in range(2):
                        hs = slice(h * 64, (h + 1) * 64)
                        for sg_ in range(NSEG):
                            bap, bkey = sbank[sg_ // 8]
                            o = bap[hs, (sg_ % 8) * 64:(sg_ % 8 + 1) * 64]
                            P.op("pe", lambda e, sg_=sg_, hs=hs, o=o: e.matmul(out=o, lhsT=BGseg[:, sg_, hs], rhs=Ub[:, hs], start=True, stop=False), r=["cl_t", "Ub"], w=[bkey])
                            P.op("pe", lambda e, sg_=sg_, hs=hs, o=o: e.matmul(out=o, lhsT=KGseg[:, sg_, hs], rhs=tok3[:, 0, hs], start=False, stop=True), r=["KGseg", tk], w=[bkey])
                    gam = t3[:, TP:TM].rearrange("p (s t) -> p s t", t=8)[:, :, 7:8]
                    P.op("dve", lambda e: e.tensor_tensor(out=Ss_n[:, :, :], in0=Ss_f[:, :, :], in1=gam.broadcast_to([128, NSEG, 64]), op=ALU.mult), r=["Ss_f", "t3"], w=["Ss_f"])
                    for half in range(2):
                        bap, bkey = sbank[half]
                        P.op("dve", lambda e, half=half, bap=bap: e.tensor_tensor(out=Ss_n[:, half * 8:(half + 1) * 8, :], in0=Ss_n[:, half * 8:(half + 1) * 8, :], in1=bap[:, :].rearrange("p (s i) -> p s i", s=8), op=ALU.add), r=["Ss_f", bkey], w=["Ss_f"])
                    P.dma("sp", wkvs_out[hp], Ss_n[:], r=["Ss_f"], sem="d_wkvs")
                else:
                    bap, bkey = bank(7)
                    for h in range(2):
                        hs = slice(h * 64, (h + 1) * 64)
                        o = bap[hs, 0:64]
                        P.op("pe", lambda e, hs=hs, o=o: e.matmul(out=o, lhsT=tok3[:, 1, hs], rhs=Ub[:, hs], start=True, stop=False), r=[tk, "Ub"], w=[bkey])
                        P.op("pe", lambda e, hs=hs, o=o: e.matmul(out=o, lhsT=tok3[:, 2, hs], rhs=tok3[:, 0, hs], start=False, stop=True), r=[tk], w=[bkey])
                    P.op("dve", lambda e, c=c, bap=bap: e.scalar_tensor_tensor(out=S_f[:, hp, :], in0=S_f[:, hp, :], scalar=t3[:, c * 128 + 127:c * 128 + 128], in1=bap[:, 0:64], op0=ALU.mult, op1=ALU.add), r=["S_f", "t3", bkey], w=["S_f"])
                    P.op("act", lambda e: e.activation(out=S_b[:, :], in_=S_f[:, hp, :], func=AF.Copy), r=["S_f"], w=["S_b"])

            groups = [[0, 1, 2, 3, 4], [5, 6, 7, 8]] if is_main else [[0, 1, 2, 3, 4, 5, 6, 7]]
            for grp in groups:
                for slot, c in enumerate(grp):
                    chunk_p12(c, slot)
                for m in range(1, 7):
                    for slot, c in enumerate(grp):
                        chunk_level(slot)
                for slot, c in enumerate(grp):
                    chunk_chain(c, slot)
            if SUB == 4:
                return
            if is_main:
                for j in range(nbt):
                    c0, c1 = j * 512, min(TY, j * 512 + 512)
                    bap, bkey = bank(0 + j)
                    P.op("pe", lambda e, c0=c0, c1=c1, bap=bap: e.matmul(out=bap[:, 0:c1 - c0], lhsT=blks[:, :], rhs=yT[:, c0:c1], start=True, stop=True), r=["blks", "a_t"], w=[bkey])
                P.op("dve", lambda e: e.tensor_tensor(out=yT[:, :], in0=yT[:, :], in1=zpa, op=ALU.subtract), r=["a_t"] + zkeysa, w=["a_t"])
                P.op("act", lambda e: e.activation(out=t2[:, :], in_=yT[:, :], func=AF.Square), r=["a_t"], w=["t2"])
                for j in range(nbt):
                    c0, c1 = j * 512, min(TY, j * 512 + 512)
                    bap, bkey = bank(3 + j)
                    P.op("pe", lambda e, c0=c0, c1=c1, bap=bap: e.matmul(out=bap[:, 0:c1 - c0], lhsT=blks[:, :], rhs=t2[:, c0:c1], start=True, stop=True), r=["blks", "t2"], w=[bkey])
                P.op("act", lambda e: e.activation(out=t2[:, :], in_=zp, func=AF.Sqrt, bias=EPS_GN(), scale=1.0), r=zkeys + ["pp"], w=["t2"])
                P.op("dve", lambda e: e.reciprocal(out=t2[:, :], in_=t2[:, :]), r=["t2"], w=["t2"])
                P.op("dve", lambda e: e.tensor_tensor(out=yT[:, :], in0=yT[:, :], in1=t2[:, :], op=ALU.mult), r=["a_t", "t2"], w=["a_t"])
                P.op("dve", lambda e: e.tensor_scalar(out=yT[:, :], in0=yT[:, :], scalar1=ppc("lnx_w", hp, 1), scalar2=ppc("lnx_b", hp, 1), op0=ALU.mult, op1=ALU.add), r=["a_t", "pp"], w=["a_t"])
                for j in range(nbt):
                    c0, c1 = j * 512, min(TY, j * 512 + 512)
                    bap, bkey = bank(0 + j)
                    P.op("pe", lambda e, c0=c0, c1=c1, bap=bap: e.matmul(out=bap[:, 0:c1 - c0], lhsT=blk[:, :], rhs=t1[:, c0:c1], start=True, stop=True), r=["blk", "t1"], w=[bkey])
                P.op("dve", lambda e: e.tensor_tensor(out=t2[:, :], in0=v_t[:, :], in1=zpa, op=ALU.mult), r=["v_t"] + zkeysa, w=["t2"])
                P.op("dve", lambda e: e.tensor_tensor(out=yT[:, :], in0=yT[:, :], in1=t2[:, :], op=ALU.add), r=["a_t", "t2"], w=["a_t"])
                for j in range(nbt):
                    c0, c1 = j * 512, min(TY, j * 512 + 512)
                    bap, bkey = bank(3 + j)
                    for kc in range(2):
                        P.op("pe", lambda e, c0=c0, c1=c1, bap=bap, kc=kc: e.matmul(out=bap[:, 0:c1 - c0], lhsT=g2b[:, kc, :], rhs=sgT[:, kc, c0:c1], start=(kc == 0), stop=(kc == 1)), r=[g2k, "sgT"], w=[bkey])
                P.op("dve", lambda e: e.tensor_tensor(out=ob[:, :], in0=yT[:, :], in1=zp, op=ALU.mult), r=["a_t"] + zkeys, w=["KK"])
                P.dma("sp", catT[f0:f0 + 128, :], ob[:, :], r=["KK"], sem="d_cat")
        if is_main:
            P.dma("sp", wkvp_out.rearrange("h p i -> p h i"), S_f[:, :, :], r=["S_f"], sem="d_wkvp")

    def conv_phase(st, xT, xkey, E):
        nonlocal wslots
        wslots = [sb(st, "cws%d" % i, [128, 32, 128], BF16) for i in range(2)]
        wstate["i"] = 0
        scT = sb(st, "scT", [128, 16, NSEG, 30])
        st0 = ExitStack()
        sct = [sb(st0, "sct%d" % i, [120, DR]) for i in range(2)]
        for rt in range(4):
            s = rt % 2
            P.dma("sp", sct[s][:, :], sconv[rt * 120:(rt + 1) * 120, :], w=["sct%d" % s])
            for cc in range(16):
                bap, bkey = bank(6 + cc % 2)
                P.op("pe", lambda e, s=s, cc=cc, bap=bap: e.transpose(out=bap[:, 0:120], in_=sct[s][0:120, cc * 128:(cc + 1) * 128], identity=ident_f[0:120, 0:120]), r=["sct%d" % s, "ident_f"], w=[bkey])
                P.op("act" if cc % 2 == 0 else "dve",
                     (lambda e, cc=cc, rt=rt, bap=bap: e.activation(out=scT[:, cc, rt * 4:(rt + 1) * 4, :], in_=bap[:, 0:120].rearrange("p (s t) -> p s t", t=30), func=AF.Copy)) if cc % 2 == 0 else
                     (lambda e, cc=cc, rt=rt, bap=bap: e.tensor_copy(out=scT[:, cc, rt * 4:(rt + 1) * 4, :], in_=bap[:, 0:120].rearrange("p (s t) -> p s t", t=30))),
                     r=[bkey], w=["scT"])
        P.barrier()
        st0.close()
        P.dma("sp", convs_out[:, 0:22, :], sconv.rearrange("(s t) c -> s t c", t=30)[:, 8:30, :], sem="d_cvcp")
        utok = sb(st, "utok", [128, DR])
        ulast = sb(st, "ulast", [30, DR])
        sum1 = sb(st, "sum1", [128, TM])
        sum2 = sb(st, "sum2", [128, TM])
        onesc = sb(st, "onesc", [128, 128])
        P.dma("sp", onesc[:], onesc_d, w=["onesc"])
        sl = ExitStack()
        uXs = [sb(sl, "uX%d" % i, [128, E]) for i in range(2)]
        sgs = [sb(sl, "sg%d" % i, [128, E]) for i in range(2)]
        exts = [sb(sl, "ext%d" % i, [128, NSEG, 38]) for i in range(2)]
        accs_ = [sb(sl, "acc%d" % i, [128, TM]) for i in range(2)]
        sqcs = [sb(sl, "sqc%d" % i, [128, TM]) for i in range(2)]
        usm = sb(sl, "usm", [128, 128])
        nbt = 3
        for cc in range(16):
            pb = cc % 2
            uX, sg, ext, acc, sqc = uXs[pb], sgs[pb], exts[pb], accs_[pb], sqcs[pb]
            kuX, ksg, kext, kacc, ksqc = "uX%d" % pb, "sg%d" % pb, "ext%d" % pb, "acc%d" % pb, "sqc%d" % pb
            wv, wvk = load_wcol(RW + cc * 128, 128)
            gemm_fm(wv, wvk, 128, xT, xkey, E, 0)
            wg, wgk = load_wcol(RW + DR + cc * 128, 128)
            gemm_fm(wg, wgk, 128, xT, xkey, E, 3)
            zv, zvk = zps(0, 128, E)
            zg, zgk = zps(3, 128, E)
            P.op("act", lambda e: e.activation(out=sg[:, :], in_=zg, func=AF.Sigmoid), r=zgk, w=[ksg])
            P.op("dve", lambda e: e.tensor_tensor(out=uX[:, :], in0=zv, in1=sg[:, :], op=ALU.mult), r=zvk + [ksg], w=[kuX])
            cw = lambda k, cc=cc: pp[:, PP["conv_w"][0] + cc * 31 + k:PP["conv_w"][0] + cc * 31 + k + 1]
            P.op("dve", lambda e: e.tensor_scalar(out=acc[:, 0:TP], in0=uX[:, 2:2 + TP], scalar1=cw(0), scalar2=ppc("conv_b", cc, 1), op0=ALU.mult, op1=ALU.add), r=[kuX, "pp"], w=[kacc])
            for k in range(1, 31):
                P.op("dve", lambda e, k=k: e.scalar_tensor_tensor(out=acc[:, 0:TP], in0=uX[:, 2 + k:2 + k + TP], scalar=cw(k), in1=acc[:, 0:TP], op0=ALU.mult, op1=ALU.add), r=[kuX, "pp", kacc], w=[kacc])
            P.op("act", lambda e: e.activation(out=ext[:, :, 0:30], in_=scT[:, cc, :, :], func=AF.Copy), r=["scT"], w=[kext])
            P.op("act", lambda e: e.activation(out=ext[:, :, 30:38], in_=uX[:, 32 + TP:E].rearrange("p (s t) -> p s t", t=9)[:, :, 1:9], func=AF.Copy), r=[kuX], w=[kext])
            accs = acc[:, TP:TM].rearrange("p (s t) -> p s t", t=8)
            P.op("dve", lambda e: e.tensor_scalar(out=accs, in0=ext[:, :, 0:8], scalar1=cw(0), scalar2=ppc("conv_b", cc, 1), op0=ALU.mult, op1=ALU.add), r=[kext, "pp"], w=[kacc])
            for k in range(1, 31):
                P.op("dve", lambda e, k=k: e.scalar_tensor_tensor(out=accs, in0=ext[:, :, k:k + 8], scalar=cw(k), in1=accs, op0=ALU.mult, op1=ALU.add), r=[kext, "pp", kacc], w=[kacc])
            P.dma("sp", cTd[cc * 128:(cc + 1) * 128, :], acc[:, :], r=[kacc], sem="d_cTd")
            P.op("act", lambda e: e.activation(out=sqc[:, :], in_=acc[:, :], func=AF.Square), r=[kacc], w=[ksqc])
            for (srcq, skey, dstq, dkey, b0) in [(acc, kacc, sum1, "sum1", 0), (sqc, ksqc, sum2, "sum2", 3)]:
                for j in range(nbt):
                    c0, c1 = j * 512, min(TM, j * 512 + 512)
                    bap, bkey = bank(b0 + j)
                    P.op("pe", lambda e, c0=c0, c1=c1, bap=bap, srcq=srcq: e.matmul(out=bap[:, 0:c1 - c0], lhsT=onesc[:, :], rhs=srcq[:, c0:c1], start=True, stop=True), r=["onesc", skey], w=[bkey])
                zz = (zA if b0 == 0 else zB)[:, 0:TM]
                zzk = ["ps%d" % (b0 + j) for j in range(nbt)]
                if cc == 0:
                    P.op("act", lambda e, zz=zz, dstq=dstq: e.activation(out=dstq[:, :], in_=zz, func=AF.Copy), r=zzk, w=[dkey])
                else:
                    P.op("dve", lambda e, zz=zz, dstq=dstq: e.tensor_tensor(out=dstq[:, :], in0=dstq[:, :], in1=zz, op=ALU.add), r=zzk + [dkey], w=[dkey])
            P.op("act", lambda e: e.activation(out=usm[:, :].rearrange("p (s t) -> p s t", t=8), in_=ext[:, :, 30:38], func=AF.Copy), r=[kext], w=["usm"])
            bap, bkey = bank(6)
            P.op("pe", lambda e, bap=bap: e.transpose(out=bap[:, 0:128], in_=usm[:, :], identity=ident_f[:, :]), r=["usm", "ident_f"], w=[bkey])
            P.op("act", lambda e, cc=cc, bap=bap: e.activation(out=utok[:, cc * 128:(cc + 1) * 128], in_=bap[:, 0:128], func=AF.Copy), r=[bkey], w=["utok"])
            bap7, bkey7 = bank(7)
            P.op("pe", lambda e, bap7=bap7: e.transpose(out=bap7[0:30, 0:128], in_=uX[:, 32 + TP - 30:32 + TP], identity=ident_f[:, :]), r=[kuX, "ident_f"], w=[bkey7])
            P.op("act", lambda e, cc=cc, bap7=bap7: e.activation(out=ulast[0:30, cc * 128:(cc + 1) * 128], in_=bap7[0:30, 0:128], func=AF.Copy), r=[bkey7], w=["ulast"])
        for s_ in range(NSEG):
            P.dma("sp", convs_out[s_, 22:30, :], utok[s_ * 8:(s_ + 1) * 8, :], r=["utok"], sem="d_cvs%d" % (s_ % 4))
        P.dma("sp", convp_out[:, :], ulast[:, :], r=["ulast"], sem="d_cvp")
        P.barrier()
        sl.close()
        sqc = sb(st, "sqcf", [128, TM])
        P.op("dve", lambda e: e.tensor_tensor(out=sqc[:, :], in0=sum1[:, :], in1=sum1[:, :], op=ALU.mult), r=["sum1"], w=["sqc"])
        P.op("dve", lambda e: e.tensor_tensor(out=sum2[:, :], in0=sum2[:, :], in1=sqc[:, :], op=ALU.subtract), r=["sum2", "sqc"], w=["sum2"])
        P.op("act", lambda e: e.activation(out=sum2[:, :], in_=sum2[:, :], func=AF.Sqrt, bias=EPS_LN(), scale=1.0), r=["sum2", "pp"], w=["sum2"])
        P.op("dve", lambda e: e.reciprocal(out=sum2[:, :], in_=sum2[:, :]), r=["sum2"], w=["sum2"])
        cin = [sb(st, "cin%d" % i, [128, TM]) for i in range(2)]
        cob = [sb(st, "cob%d" % i, [128, TM], BF16) for i in range(2)]
        for cc in range(16):
            s = cc % 2
            wait_dram("d_cTd")
            P.dma("sp", cin[s][:, :], cTd[cc * 128:(cc + 1) * 128, :], w=["cin%d" % s])
            P.op("dve", lambda e, s=s: e.tensor_tensor(out=cin[s][:, :], in0=cin[s][:, :], in1=sum1[:, :], op=ALU.subtract), r=["cin%d" % s, "sum1"], w=["cin%d" % s])
            P.op("dve", lambda e, s=s: e.tensor_tensor(out=cin[s][:, :], in0=cin[s][:, :], in1=sum2[:, :], op=ALU.mult), r=["cin%d" % s, "sum2"], w=["cin%d" % s])
            P.op("act", lambda e, s=s, cc=cc: e.activation(out=cob[s][:, :], in_=cin[s][:, :], func=AF.Silu, bias=ppc("cln_b", cc, 1), scale=ppc("cln_w", cc, 1)), r=["cin%d" % s, "pp"], w=["cob%d" % s])
            P.dma("sp", catT[DR + cc * 128:DR + (cc + 1) * 128, :], cob[s][:, :], r=["cob%d" % s], sem="d_cat")

    def load_actT(st, name, src_d, nk, tok0, ntok):
        t = sb(st, name, [128, nk, ntok], BF16)
        for k0 in range(0, nk, 16):
            k1 = min(nk, k0 + 16)
            P.dma("sp", t[:, k0:k1, :], src_d[k0 * 128:k1 * 128, tok0:tok0 + ntok].rearrange("(k p) n -> p k n", p=128), w=[name], sem="d_%s_%d" % (name, (k0 // 16) % 4))
        return t

    def wait_dram(semname, q="sp"):
        if semname in P.n2p:
            pn = P.n2p[semname]
            if P.dcnt[pn] > 0:
                for eng in ("sp", "pool"):
                    P._wait(eng, ("d", pn, P.dcnt[pn]))

    def out_phase():
        with ExitStack() as st:
            wait_dram("d_cat")
            aT = load_actT(st, "caT", catT, 32, 0, TM)
            ws = [sb(st, "wo%d" % i, [128, 32, 512], BF16) for i in range(2)]
            xin = [sb(st, "oxin%d" % i, [128, 512]) for i in range(3)]
            ho = [sb(st, "oho%d" % i, [128, 512]) for i in range(3)]
            sqj = sb(st, "osq", [128, 512])
            ssp = sb(st, "ossp", [128, 9, 8])
            it = 0
            for nb in range(8):
                s = nb % 2
                P.dma("pool", ws[s][:, :, :], w_out[:, nb * 512:(nb + 1) * 512].rearrange("(k p) n -> p k n", p=128), w=["wo%d" % s])
                for tt in range(9):
                    bap, bkey = bank(it % 6)
                    xs = it % 3
                    it += 1
                    P.dma("sp", xin[xs][:, :], x_main[tt * 128:(tt + 1) * 128, nb * 512:(nb + 1) * 512], w=["oxin%d" % xs])
                    for k in range(32):
                        P.op("pe", lambda e, k=k, s=s, tt=tt, bap=bap: e.matmul(out=bap[:, :], lhsT=aT[:, k, tt * 128:(tt + 1) * 128], rhs=ws[s][:, k, :], start=(k == 0), stop=(k == 31)), r=["caT", "wo%d" % s], w=[bkey])
                    P.op("dve", lambda e, xs=xs, bap=bap: e.tensor_tensor(out=ho[xs][:, :], in0=bap[:, :], in1=xin[xs][:, :], op=ALU.add), r=[bkey, "oxin%d" % xs], w=["oho%d" % xs])
                    P.op("act", lambda e, xs=xs: e.activation(out=sqj[:, :], in_=ho[xs][:, :], func=AF.Square), r=["oho%d" % xs], w=["osq"])
                    P.op("dve", lambda e, tt=tt, nb=nb: e.reduce_sum(out=ssp[:, tt, nb:nb + 1], in_=sqj[:, :], axis=AX.X), r=["osq"], w=["ossp"])
                    P.dma("sp", h1d[tt * 128:(tt + 1) * 128, nb * 512:(nb + 1) * 512], ho[xs][:, :], r=["oho%d" % xs], sem="d_h1_%d" % xs)
            P.op("dve", lambda e: e.reduce_sum(out=ss_all[:, 0:9], in_=ssp[:, :, :], axis=AX.X), r=["ossp"], w=["ss_all"])
            P.barrier()

    ss_all = sb(es, "ss_all", [128, 32])

    def norm_from_dram(st, src_d, sems, gcol, ss_col0, xT, xkey, tag):
        for sname in sems:
            wait_dram(sname)
        gb = sb(st, tag + "_gb", [128, D])
        P.dma("sp", gb[:], gvec[gcol].partition_broadcast(128), w=[tag + "_gb"])
        xt = [sb(st, tag + "_xt%d" % i, [128, D]) for i in range(2)]
        xnb = [sb(st, tag + "_xnb%d" % i, [128, D], BF16) for i in range(2)]
        rs = sb(st, tag + "_rs", [128, 9])
        P.op("act", lambda e: e.activation(out=rs[:, :], in_=ss_all[:, ss_col0:ss_col0 + 9], func=AF.Sqrt, bias=EPS_RMS(), scale=1.0 / D), r=["ss_all", "pp"], w=[tag + "_rs"])
        P.op("dve", lambda e: e.reciprocal(out=rs[:, :], in_=rs[:, :]), r=[tag + "_rs"], w=[tag + "_rs"])
        for tt in range(9):
            s = tt % 2
            P.dma("sp", xt[s][:, :], src_d[tt * 128:(tt + 1) * 128, :], w=[tag + "_xt%d" % s])
            P.op("dve", lambda e, s=s, tt=tt: e.scalar_tensor_tensor(out=xnb[s][:, :], in0=xt[s][:, :], scalar=rs[:, tt:tt + 1], in1=gb[:, :], op0=ALU.mult, op1=ALU.mult), r=[tag + "_xt%d" % s, tag + "_rs", tag + "_gb"], w=[tag + "_xnb%d" % s])
            transpose_tile(xnb[s], tag + "_xnb%d" % s, 128, lambda kk, tt=tt: xT[:, kk * 8:(kk + 1) * 8, tt * 128:(tt + 1) * 128], xkey)

    def ffn_up_phase():
        with ExitStack() as st:
            xT = sb(st, "fxT", [128, 32, TM], BF16)
            with ExitStack() as sa:
                norm_from_dram(sa, h1d, ["d_h1_0", "d_h1_1", "d_h1_2"], 1, 0, xT, "fxT", "fn")
                P.barrier()
            wsl = [sb(st, "fw%d" % i, [128, 32, 128], BF16) for i in range(6)]
            sgl = sb(st, "fsgl", [128, TM])
            ao = [sb(st, "fao%d" % i, [128, TM], BF16) for i in range(2)]
            li = 0
            for fc in range(NFC):
                sg_i, su_i = (li % 6), ((li + 1) % 6)
                li += 2
                P.dma("pool", wsl[sg_i][:, :, :], w_gu[:, fc * 128:(fc + 1) * 128].rearrange("(k p) n -> p k n", p=128), w=["fw%d" % sg_i])
                P.dma("pool", wsl[su_i][:, :, :], w_gu[:, DFF + fc * 128:DFF + (fc + 1) * 128].rearrange("(k p) n -> p k n", p=128), w=["fw%d" % su_i])
                gemm_fm(wsl[sg_i], "fw%d" % sg_i, 128, xT, "fxT", TM, 0)
                gemm_fm(wsl[su_i], "fw%d" % su_i, 128, xT, "fxT", TM, 3)
                s = fc % 2
                P.op("act", lambda e: e.activation(out=sgl[:, :], in_=zA[:, 0:TM], func=AF.Silu), r=["ps0", "ps1", "ps2"], w=["fsgl"])
                P.op("dve", lambda e, s=s: e.tensor_tensor(out=ao[s][:, :], in0=zB[:, 0:TM], in1=sgl[:, :], op=ALU.mult), r=["ps3", "ps4", "ps5", "fsgl"], w=["fao%d" % s])
                P.dma("sp", actT[fc * 128:(fc + 1) * 128, :], ao[s][:, :], r=["fao%d" % s], sem="d_act%d" % s)
            P.barrier()

    def tm_gemm_phase(tag, aT_d, a_sems, nk, Wd, res_d, res_sems, out_d, out_semtag, ss_col0, groups, wcols=512, nslots=4):
        for sname in a_sems + res_sems:
            wait_dram(sname)
        for gi, (t0, nt) in enumerate(groups):
            with ExitStack() as st:
                aT = load_actT(st, "%saT%d" % (tag, gi), aT_d, nk, t0 * 128, nt * 128)
                akey = "%saT%d" % (tag, gi)
                nhalf = 2
                kh = (nk + 1) // 2
                ws = [sb(st, "%sw%d_%d" % (tag, gi, i), [128, kh, wcols], BF16) for i in range(nslots)]
                xin = [sb(st, "%sxin%d_%d" % (tag, gi, i), [128, wcols]) for i in range(3)]
                ho = [sb(st, "%sho%d_%d" % (tag, gi, i), [128, wcols]) for i in range(3)]
                sqj = sb(st, "%ssq%d" % (tag, gi), [128, wcols])
                nblk = D // wcols
                ssp = sb(st, "%sssp%d" % (tag, gi), [128, nt, nblk])
                it = 0
                li = 0
                for nb in range(nblk):
                    slots = []
                    for hf in range(2):
                        si = li % nslots
                        li += 1
                        k0, k1 = hf * kh, min(nk, (hf + 1) * kh)
                        P.dma("pool", ws[si][:, 0:k1 - k0, :], Wd[k0 * 128:k1 * 128, nb * wcols:(nb + 1) * wcols].rearrange("(k p) n -> p k n", p=128), w=["%sw%d_%d" % (tag, gi, si)])
                        slots.append((si, k0, k1))
                    for tl in range(nt):
                        tt = t0 + tl
                        bap, bkey = bank(it % 6)
                        xs = it % 3
                        it += 1
                        P.dma("sp", xin[xs][:, :], res_d[tt * 128:(tt + 1) * 128, nb * wcols:(nb + 1) * wcols], w=["%sxin%d_%d" % (tag, gi, xs)])
                        for (si, k0, k1) in slots:
                            for k in range(k0, k1):
                                P.op("pe", lambda e, k=k, k0=k0, si=si, tl=tl, bap=bap: e.matmul(out=bap[:, 0:wcols], lhsT=aT[:, k, tl * 128:(tl + 1) * 128], rhs=ws[si][:, k - k0, :], start=(k == 0), stop=(k == nk - 1)), r=[akey, "%sw%d_%d" % (tag, gi, si)], w=[bkey])
                        P.op("dve", lambda e, xs=xs, bap=bap: e.tensor_tensor(out=ho[xs][:, :], in0=bap[:, 0:wcols], in1=xin[xs][:, :], op=ALU.add), r=[bkey, "%sxin%d_%d" % (tag, gi, xs)], w=["%sho%d_%d" % (tag, gi, xs)])
                        P.op("act", lambda e, xs=xs: e.activation(out=sqj[:, :], in_=ho[xs][:, :], func=AF.Square), r=["%sho%d_%d" % (tag, gi, xs)], w=["%ssq%d" % (tag, gi)])
                        P.op("dve", lambda e, tl=tl, nb=nb: e.reduce_sum(out=ssp[:, tl, nb:nb + 1], in_=sqj[:, :], axis=AX.X), r=["%ssq%d" % (tag, gi)], w=["%sssp%d" % (tag, gi)])
                        P.dma("sp", out_d[tt * 128:(tt + 1) * 128, nb * wcols:(nb + 1) * wcols], ho[xs][:, :], r=["%sho%d_%d" % (tag, gi, xs)], sem="%s_%d" % (out_semtag, xs))
                P.op("dve", lambda e: e.reduce_sum(out=ss_all[:, ss_col0 + t0:ss_col0 + t0 + nt], in_=ssp[:, :, :], axis=AX.X), r=["%sssp%d" % (tag, gi)], w=["ss_all"])
                P.barrier()

    def ple_phase():
        with ExitStack() as st:
            xT = sb(st, "plxT", [128, 32, TM], BF16)
            peT = sb(st, "peT", [128, 2, TM], BF16)
            with ExitStack() as sa:
                norm_from_dram(sa, h2d, ["d_h2_0", "d_h2_1", "d_h2_2"], 2, 9, xT, "plxT", "pn")
                pin = [sb(sa, "pin%d" % i, [128, PLE]) for i in range(2)]
                pinb = [sb(sa, "pinb%d" % i, [128, PLE], BF16) for i in range(2)]
                for tt in range(9):
                    s = tt % 2
                    P.dma("sp", pin[s][:, :], p_main[tt * 128:(tt + 1) * 128, :], w=["pin%d" % s])
                    P.op("dve", lambda e, s=s: e.tensor_copy(out=pinb[s][:, :], in_=pin[s][:, :]), r=["pin%d" % s], w=["pinb%d" % s])
                    bap, bkey = bank(6 + tt % 2)
                    pT = bap.bitcast(BF16)
                    for q in range(2):
                        P.op("pe", lambda e, s=s, q=q, pT=pT: e.transpose(out=pT[:, q * 128:(q + 1) * 128], in_=pinb[s][:, q * 128:(q + 1) * 128], identity=ident_b[:, :]), r=["pinb%d" % s, "ident_b"], w=[bkey])
                    P.op("act", lambda e, tt=tt, pT=pT: e.activation(out=peT[:, :, tt * 128:(tt + 1) * 128], in_=pT[:, 0:256].rearrange("p (q t) -> p q t", q=2), func=AF.Copy), r=[bkey], w=["peT"])
                P.barrier()
            ws = [sb(st, "pw%d" % i, [128, 32, 512], BF16) for i in range(2)]
            wp = [sb(st, "pwp%d" % i, [128, 2, 512], BF16) for i in range(2)]
            xin = [sb(st, "pxin%d" % i, [128, 512]) for i in range(3)]
            ho = [sb(st, "pho%d" % i, [128, 512]) for i in range(3)]
            pgs = sb(st, "pgs", [128, 512])
            sqj = sb(st, "psq", [128, 512])
            ssp = sb(st, "pssp", [128, 9, 8])
            it = 0
            for nb in range(8):
                s = nb % 2
                P.dma("pool", ws[s][:, :, :], w_pg[:, nb * 512:(nb + 1) * 512].rearrange("(k p) n -> p k n", p=128), w=["pw%d" % s])
                P.dma("pool", wp[s][:, :, :], w_pp[:, nb * 512:(nb + 1) * 512].rearrange("(k p) n -> p k n", p=128), w=["pwp%d" % s])
                for tt in range(9):
                    b1, k1 = bank((2 * it) % 6)
                    b2, k2 = bank((2 * it + 1) % 6)
                    xs = it % 3
                    it += 1
                    P.dma("sp", xin[xs][:, :], h2d[tt * 128:(tt + 1) * 128, nb * 512:(nb + 1) * 512], w=["pxin%d" % xs])
                    for k in range(32):
                        P.op("pe", lambda e, k=k, s=s, tt=tt, b1=b1: e.matmul(out=b1[:, :], lhsT=xT[:, k, tt * 128:(tt + 1) * 128], rhs=ws[s][:, k, :], start=(k == 0), stop=(k == 31)), r=["plxT", "pw%d" % s], w=[k1])
                    for k in range(2):
                        P.op("pe", lambda e, k=k, s=s, tt=tt, b2=b2: e.matmul(out=b2[:, :], lhsT=peT[:, k, tt * 128:(tt + 1) * 128], rhs=wp[s][:, k, :], start=(k == 0), stop=(k == 1)), r=["peT", "pwp%d" % s], w=[k2])
                    P.op("act", lambda e, b1=b1: e.activation(out=pgs[:, :], in_=b1[:, :], func=AF.Sigmoid), r=[k1], w=["pgs"])
                    P.op("dve", lambda e, b2=b2: e.tensor_tensor(out=pgs[:, :], in0=pgs[:, :], in1=b2[:, :], op=ALU.mult), r=["pgs", k2], w=["pgs"])
                    P.op("dve", lambda e, xs=xs: e.tensor_tensor(out=ho[xs][:, :], in0=pgs[:, :], in1=xin[xs][:, :], op=ALU.add), r=["pgs", "pxin%d" % xs], w=["pho%d" % xs])
                    P.op("act", lambda e, xs=xs: e.activation(out=sqj[:, :], in_=ho[xs][:, :], func=AF.Square), r=["pho%d" % xs], w=["psq"])
                    P.op("dve", lambda e, tt=tt, nb=nb: e.reduce_sum(out=ssp[:, tt, nb:nb + 1], in_=sqj[:, :], axis=AX.X), r=["psq"], w=["pssp"])
                    P.dma("sp", h3d[tt * 128:(tt + 1) * 128, nb * 512:(nb + 1) * 512], ho[xs][:, :], r=["pho%d" % xs], sem="d_h3_%d" % xs)
            P.op("dve", lambda e: e.reduce_sum(out=ss_all[:, 18:27], in_=ssp[:, :, :], axis=AX.X), r=["pssp"], w=["ss_all"])
            P.barrier()

    def final_phase():
        with ExitStack() as st:
            for sname in ["d_h3_0", "d_h3_1", "d_h3_2"]:
                wait_dram(sname)
            gb = sb(st, "fgb", [128, D])
            P.dma("sp", gb[:], gvec[3].partition_broadcast(128), w=["fgb"])
            xt = [sb(st, "fxt%d" % i, [128, D]) for i in range(2)]
            yo = [sb(st, "fyo%d" % i, [128, D]) for i in range(2)]
            rs = sb(st, "frs", [128, 9])
            P.op("act", lambda e: e.activation(out=rs[:, :], in_=ss_all[:, 18:27], func=AF.Sqrt, bias=EPS_RMS(), scale=1.0 / D), r=["ss_all", "pp"], w=["frs"])
            P.op("dve", lambda e: e.reciprocal(out=rs[:, :], in_=rs[:, :]), r=["frs"], w=["frs"])
            for tt in range(9):
                s = tt % 2
                P.dma("sp", xt[s][:, :], h3d[tt * 128:(tt + 1) * 128, :], w=["fxt%d" % s])
                P.op("dve", lambda e, s=s, tt=tt: e.scalar_tensor_tensor(out=yo[s][:, :], in0=xt[s][:, :], scalar=rs[:, tt:tt + 1], in1=gb[:, :], op0=ALU.mult, op1=ALU.mult), r=["fxt%d" % s, "frs", "fgb"], w=["fyo%d" % s])
                P.dma("sp", y_out[tt * 128:(tt + 1) * 128, :], yo[s][:, :], r=["fyo%d" % s], sem="d_yout%d" % s)
            P.barrier()

    nph = NPH
    if SUB == 10:
        nph = 0
    if nph >= 1:
        mixer_pass(False)
    if nph >= 2:
        mixer_pass(True)
    if nph >= 3:
        out_phase()
    if nph >= 4:
        ffn_up_phase()
    if nph >= 5:
        tm_gemm_phase("dn", actT, ["d_act0", "d_act1"], NFC, w_dn, h1d, ["d_h1_0", "d_h1_1", "d_h1_2"], h2d, "d_h2", 9, [(0, 5), (5, 4)], wcols=256, nslots=3)
    if nph >= 6:
        ple_phase()
    if nph >= 7:
        final_phase()
    P.barrier()
    es.close()
    return nc, P


_CACHE = {}


def _consts():
    ii = np.arange(128)
    part = ii[:, None]
    free = ii[None, :]
    U = (part < free).astype(np.float32)
    Ui = (part <= free).astype(np.float32)
    L = (part > free).astype(np.float32)
    same = ((part // 8) == (free // 8)).astype(np.float32)
    maskg = np.stack([np.concatenate([-U, Ui, U, Ui], axis=1),
                      np.concatenate([-U * same, Ui * same, U * same, Ui * same], axis=1)]).astype(np.float32)
    maskl = np.stack([np.concatenate([-L, -L], axis=1), np.concatenate([-L * same, -L * same], axis=1)]).astype(np.float32)
    blk = np.zeros((128, 128), np.float32)
    blk[:64, :64] = 1
    blk[64:, 64:] = 1
    segt = ((ii[:, None] // 8) == np.arange(NSEG)[None, :]).astype(np.float32)
    segf = np.broadcast_to(segt.T[None, :, :], (128, NSEG, 128)).reshape(128, NSEG * 128).astype(np.float32)
    rm = np.ones((128, TM), np.float32)
    rm[:, 0:TP:128] = 0
    rm[:, TP:TM:8] = 0
    return dict(ident=np.eye(128, dtype=np.float32), blk=blk, blks=blk / 64.0,
                onesc=np.full((128, 128), 1.0 / DR, np.float32), maskg=maskg, maskl=maskl,
                segf=np.ascontiguousarray(segf), segt=segt, rmask=rm)


def make_in_maps(inp, cores=None):
    f = lambda k: np.asarray(inp[k], dtype=np.float32)
    x_prompt, x_sample = f("x_prompt"), f("x_sample")
    state_wkv, state_shift, state_conv = f("state_wkv")[0], f("state_shift")[0], f("state_conv")[0]
    p_prompt, p_sample = f("p_prompt")[0], f("p_sample")[0]

    def col(v, n):
        return np.ascontiguousarray(np.asarray(v, np.float32).reshape(n, 128).T)
    pp = np.zeros((128, NPP), np.float32)

    def put(name, arr):
        o, w = PP[name]
        pp[:arr.shape[0], o:o + arr.shape[1]] = arr
    mu = f("mu_shift")[0]
    put("mu_r", col(mu[0:DR], 16))
    put("mu_k", col(mu[DR:2 * DR], 16))
    put("mu_v", col(mu[2 * DR:3 * DR], 16))
    put("mu_w", mu[3 * DR:3 * DR + 96].reshape(96, 1))
    put("mu_a", mu[3 * DR + 96:3 * DR + 192].reshape(96, 1))
    put("mu_g", col(mu[3 * DR + 192:3 * DR + 448], 2))
    for nm, key in [("w0", "w0"), ("a0", "a0"), ("k_k", "k_k"), ("k_a", "k_a"), ("lnx_w", "lnx_w"), ("lnx_b", "lnx_b"),
                    ("conv_b", "conv_b"), ("cln_w", "conv_ln_w"), ("cln_b", "conv_ln_b")]:
        put(nm, col(f(key)[0], 16))
    put("r_k", col(f("r_k")[0].reshape(-1), 16))
    cw = f("conv_w")[0]
    put("conv_w", np.ascontiguousarray(cw.T.reshape(16, 128, 31).transpose(1, 0, 2)).reshape(128, 16 * 31))
    put("eps", np.broadcast_to(np.array([1e-6, 64e-5, 1e-5, 0.0], np.float32), (128, 4)))
    gvec = np.stack([f("g_mix")[0], f("g_ffn")[0], f("g_ple")[0], f("g_final")])
    shared = dict(w_in=f("w_in")[0], w2=f("w2")[0], a2=f("a2")[0], g2=f("g2")[0], w_out=f("w_out")[0],
                  w_gu=f("w_gate_up")[0], w_dn=f("w_down")[0], w_pg=f("w_ple_gate")[0], w_pp=f("w_ple_proj")[0],
                  gvec=gvec, pp=pp, **_consts())
    in_maps = []
    for c in (range(NCORES) if cores is None else cores):
        b, hf = c // 2, c % 2
        sl = slice(16 * c, 16 * c + 16)
        m = dict(shared)
        m["x_main"] = np.concatenate([x_prompt[b, hf * TP:(hf + 1) * TP], x_sample[sl].reshape(TS, D)], axis=0)
        m["x_pre"] = np.ascontiguousarray(x_prompt[b, 0:TP]) if hf == 1 else np.zeros((TP, D), np.float32)
        m["p_main"] = np.concatenate([p_prompt[b, hf * TP:(hf + 1) * TP], p_sample[sl].reshape(TS, PLE)], axis=0)
        m["sshift"] = np.ascontiguousarray(state_shift[sl])
        m["sconv"] = np.ascontiguousarray(state_conv[sl].reshape(NSEG * 30, DR))
        sw = state_wkv[sl].reshape(NSEG, NHP, 2, 64, 64)
        m["swkvT"] = np.ascontiguousarray(sw.transpose(1, 2, 4, 0, 3).reshape(NHP, 128, NSEG, 64))
        in_maps.append(m)
    return in_maps


def kernel(**inp):
    in_maps = make_in_maps(inp)
    if "nc" not in _CACHE:
        _CACHE["nc"] = build()[0]
    res = run_bass_kernel_spmd(_CACHE["nc"], in_maps, core_ids=list(range(NCORES)))
    R = res.results
    y_prompt = np.zeros((4, 2048, D), np.float32)
    y_sample = np.zeros((128, 8, D), np.float32)
    wkv_p = np.zeros((1, 4, 32, 64, 64), np.float32)
    shift_p = np.zeros((1, 4, D), np.float32)
    conv_p = np.zeros((1, 4, 30, DR), np.float32)
    wkv_s = np.zeros((1, 128, 32, 64, 64), np.float32)
    shift_s = np.zeros((1, 128, D), np.float32)
    conv_s = np.zeros((1, 128, 30, DR), np.float32)
    for c in range(NCORES):
        b, hf = c // 2, c % 2
        sl = slice(16 * c, 16 * c + 16)
        r = R[c]
        y_prompt[b, hf * TP:(hf + 1) * TP] = r["y_out"][0:TP]
        y_sample[sl] = r["y_out"][TP:TM].reshape(16, 8, D)
        shift_s[0, sl] = r["shift_out"][0:NSEG]
        conv_s[0, sl] = r["convs_out"]
        ws = r["wkvs_out"].reshape(NHP, 2, 64, NSEG, 64)
        wkv_s[0, sl] = ws.transpose(3, 0, 1, 4, 2).reshape(NSEG, 32, 64, 64)
        if hf == 1:
            shift_p[0, b] = r["shift_out"][NSEG]
            conv_p[0, b] = r["convp_out"]
            wp = r["wkvp_out"].reshape(NHP, 2, 64, 64)
            wkv_p[0, b] = wp.transpose(0, 1, 3, 2).reshape(32, 64, 64)
    return (y_prompt, y_sample, wkv_p, shift_p, conv_p, wkv_s, shift_s, conv_s)
```
